# Optimizing a Trainium2 kernel written in Bass

```python
import math
import jax, jax.numpy as jnp
from jax import lax
import numpy as np

D_MODEL = 4096
BATCH = 1
SEQ = 16384
DEPTH = 2

N_MEM = 256
BRANCH_W = D_MODEL // 2
A_HEADS = BRANCH_W // 128
A_NOPE = 128
A_ROPE = 64
A_VDIM = 128
A_Q_RANK = D_MODEL // 4
A_KV_RANK = 512
ROPE_THETA = 10000.0
Q_BLOCK = 128
B_W = BRANCH_W
CONV_W = 31
C_HEADS = BRANCH_W // 512
C_DK = 256
C_DV = 512
C_CHUNK = 64
F_BIAS_INIT = 3.0
D_HEADS = 4
D_HDIM = 128
D_W = D_HEADS * D_HDIM
N_BRANCH = 4
EPS = 1e-6

IN_SIZES = (
    A_Q_RANK, A_KV_RANK, A_ROPE, A_HEADS * A_VDIM,
    2 * B_W, B_W,
    C_HEADS * C_DK, C_HEADS * C_DK, C_HEADS * C_DV,
    C_HEADS * C_DV, C_HEADS * C_DV, 4 * C_HEADS,
    D_W,
    N_BRANCH * D_MODEL,
)
IN_COLS = sum(IN_SIZES)
IN_SPLITS = tuple(int(s) for s in np.cumsum(IN_SIZES)[:-1])

kernel_name = "hybrid_mla_conformer_mlstm_encoder"


def rmsnorm(x, g):
    xf = x.astype(jnp.float32)
    y = xf * lax.rsqrt(jnp.mean(xf * xf, axis=-1, keepdims=True) + EPS)
    return y.astype(x.dtype) * g


def layernorm(x, g, b):
    xf = x.astype(jnp.float32)
    xc = xf - jnp.mean(xf, axis=-1, keepdims=True)
    y = xc * lax.rsqrt(jnp.mean(xc * xc, axis=-1, keepdims=True) + EPS)
    return y.astype(x.dtype) * g + b


def rope_tables(positions):
    inv_freq = ROPE_THETA ** (-jnp.arange(0, A_ROPE, 2, dtype=jnp.float32) / A_ROPE)
    ang = positions.astype(jnp.float32)[..., None] * inv_freq
    return jnp.cos(ang), jnp.sin(ang)


def apply_rope(x, cos, sin):
    xf = x.astype(jnp.float32)
    x1, x2 = jnp.split(xf, 2, axis=-1)
    return jnp.concatenate([x1 * cos - x2 * sin, x1 * sin + x2 * cos], axis=-1).astype(x.dtype)


def mla_attention(c_q, c_kv, k_rope_in, positions, q_norm_g, w_uq, kv_norm_g, w_ukv):
    B, S, _ = c_q.shape
    cos, sin = rope_tables(positions)
    q = (rmsnorm(c_q, q_norm_g) @ w_uq).reshape(B, S, A_HEADS, A_NOPE + A_ROPE)
    q_nope, q_rope = q[..., :A_NOPE], q[..., A_NOPE:]
    q_rope = apply_rope(q_rope, cos[:, :, None, :], sin[:, :, None, :])
    k_rope = apply_rope(k_rope_in, cos, sin)
    kv = (rmsnorm(c_kv, kv_norm_g) @ w_ukv).reshape(B, S, A_HEADS, A_NOPE + A_VDIM)
    k_nope, v = kv[..., :A_NOPE], kv[..., A_NOPE:]
    scale = (A_NOPE + A_ROPE) ** -0.5
    nb = S // Q_BLOCK

    def blocks(a):
        return jnp.moveaxis(a.reshape(B, nb, Q_BLOCK, *a.shape[2:]), 1, 0)

    def attend(qs):
        qn, qr = qs
        s = (jnp.einsum('bqhd,bkhd->bhqk', qn, k_nope)
             + jnp.einsum('bqhr,bkr->bhqk', qr, k_rope)).astype(jnp.float32) * scale
        p = jax.nn.softmax(s, axis=-1).astype(v.dtype)
        return jnp.einsum('bhqk,bkhd->bqhd', p, v)

    o = lax.map(attend, (blocks(q_nope), blocks(q_rope)))
    return jnp.moveaxis(o, 0, 1).reshape(B, S, A_HEADS * A_VDIM)


def conformer_conv(u_glu, conv_w, conv_b, ln_g, ln_b):
    a, g = jnp.split(u_glu, 2, axis=-1)
    u = a * jax.nn.sigmoid(g)
    u = lax.conv_general_dilated(u, conv_w[:, None, :], window_strides=(1,),
                                 padding=[(CONV_W // 2, CONV_W // 2)],
                                 dimension_numbers=('NWC', 'WIO', 'NWC'),
                                 feature_group_count=B_W) + conv_b
    return jax.nn.silu(layernorm(u, ln_g, ln_b))


def mlstm_chunkwise(q, k, v, i_pre, f_pre):
    B, S, H, _ = q.shape
    nc = S // C_CHUNK

    def chunks(a):
        a = a.astype(jnp.float32).reshape(B, nc, C_CHUNK, H, *a.shape[3:])
        return jnp.moveaxis(jnp.moveaxis(a, 1, 0), 3, 2)

    qc, kc, vc = chunks(q), chunks(k), chunks(v)
    ic = chunks(i_pre)
    lfc = jax.nn.log_sigmoid(chunks(f_pre))
    tri = jnp.tril(jnp.ones((C_CHUNK, C_CHUNK), dtype=bool))

    def step(carry, xs):
        Cs, ns, ms = carry
        qb, kb, vb, ib, lfb = xs
        bcum = jnp.cumsum(lfb, axis=-1)
        logw = jnp.where(tri, bcum[..., :, None] - bcum[..., None, :] + ib[..., None, :], -jnp.inf)
        inter = bcum + ms[..., None]
        m_loc = jnp.maximum(inter, jnp.max(logw, axis=-1))
        w_intra = jnp.exp(logw - m_loc[..., None])
        w_inter = jnp.exp(inter - m_loc)
        s = jnp.einsum('bhjd,bhsd->bhjs', qb, kb) * w_intra
        num = (jnp.einsum('bhjs,bhsv->bhjv', s, vb)
               + w_inter[..., None] * jnp.einsum('bhjd,bhdv->bhjv', qb, Cs))
        den = jnp.sum(s, axis=-1) + w_inter * jnp.einsum('bhjd,bhd->bhj', qb, ns)
        h = num / jnp.maximum(jnp.abs(den), jnp.exp(-m_loc))[..., None]
        b_tot = bcum[..., -1]
        logw_end = b_tot[..., None] - bcum + ib
        m_new = jnp.maximum(b_tot + ms, jnp.max(logw_end, axis=-1))
        decay = jnp.exp(b_tot + ms - m_new)
        wk = jnp.exp(logw_end - m_new[..., None])[..., None] * kb
        C_new = decay[..., None, None] * Cs + jnp.einsum('bhsd,bhsv->bhdv', wk, vb)
        n_new = decay[..., None] * ns + jnp.sum(wk, axis=2)
        return (C_new, n_new, m_new), h

    init = (jnp.zeros((B, H, C_DK, C_DV), jnp.float32),
            jnp.zeros((B, H, C_DK), jnp.float32),
            jnp.zeros((B, H), jnp.float32))
    _, hs = lax.scan(step, init, (qc, kc, vc, ic, lfc))
    return jnp.moveaxis(jnp.moveaxis(hs, 2, 3), 0, 1).reshape(B, S, H, C_DV)


def mlstm_branch(q, k, v, o_pre, if_pre, gate_b, norm_g):
    B, S, _ = q.shape
    q = q.reshape(B, S, C_HEADS, C_DK)
    k = k.reshape(B, S, C_HEADS, C_DK) * (C_DK ** -0.5)
    v = v.reshape(B, S, C_HEADS, C_DV)
    g = (if_pre.reshape(B, S, 2, 2, C_HEADS) + gate_b).astype(jnp.float32)
    h_fwd = mlstm_chunkwise(q, k, v, g[:, :, 0, 0], g[:, :, 0, 1])
    flip = lambda a: jnp.flip(a, axis=1)
    h_bwd = flip(mlstm_chunkwise(flip(q), flip(k), flip(v), flip(g[:, :, 1, 0]), flip(g[:, :, 1, 1])))
    h = h_fwd + h_bwd
    h = h * lax.rsqrt(jnp.mean(h * h, axis=-1, keepdims=True) + EPS)
    h = h.reshape(B, S, C_HEADS * C_DV) * norm_g
    return (jax.nn.sigmoid(o_pre.astype(jnp.float32)) * h).astype(v.dtype)


def memory_attention(q, mem, mem_norm_g, w_mem_kv):
    B, S, _ = q.shape
    M = mem.shape[1]
    kv = rmsnorm(mem, mem_norm_g) @ w_mem_kv
    k = kv[..., :D_W].reshape(B, M, D_HEADS, D_HDIM)
    v = kv[..., D_W:].reshape(B, M, D_HEADS, D_HDIM)
    qh = q.reshape(B, S, D_HEADS, D_HDIM)
    s = jnp.einsum('bshd,bmhd->bhsm', qh, k).astype(jnp.float32) * (D_HDIM ** -0.5)
    p = jax.nn.softmax(s, axis=-1).astype(v.dtype)
    return jnp.einsum('bhsm,bmhd->bshd', p, v).reshape(B, S, D_W)


def setup_inputs(seed: int = 0) -> dict:
    key = jax.random.key(seed)
    ks = jax.random.split(key, 24)
    f32 = jnp.float32
    nrm = lambda k, shape, scale: jax.random.normal(k, shape, f32) * scale
    gain = lambda k, shape: 1.0 + 0.05 * jax.random.normal(k, shape, f32)
    gate_offset = jnp.array([0.0, F_BIAS_INIT], f32)[None, None, :, None]
    return {
        "x": nrm(ks[0], (BATCH, SEQ, D_MODEL), 1.0),
        "mem": nrm(ks[1], (BATCH, N_MEM, D_MODEL), 1.0),
        "positions": jnp.broadcast_to(jnp.arange(SEQ, dtype=jnp.int32)[None, :], (BATCH, SEQ)),
        "norm_g": gain(ks[2], (DEPTH, D_MODEL)),
        "w_in": nrm(ks[3], (DEPTH, D_MODEL, IN_COLS), D_MODEL ** -0.5),
        "mla_cq_norm_g": gain(ks[4], (DEPTH, A_Q_RANK)),
        "w_uq": nrm(ks[5], (DEPTH, A_Q_RANK, A_HEADS * (A_NOPE + A_ROPE)), A_Q_RANK ** -0.5),
        "mla_ckv_norm_g": gain(ks[6], (DEPTH, A_KV_RANK)),
        "w_ukv": nrm(ks[7], (DEPTH, A_KV_RANK, A_HEADS * (A_NOPE + A_VDIM)), A_KV_RANK ** -0.5),
        "conv_w": nrm(ks[8], (DEPTH, CONV_W, B_W), CONV_W ** -0.5),
        "conv_b": nrm(ks[9], (DEPTH, B_W), 0.02),
        "conv_ln_g": gain(ks[10], (DEPTH, B_W)),
        "conv_ln_b": nrm(ks[11], (DEPTH, B_W), 0.02),
        "mlstm_gate_b": gate_offset + nrm(ks[12], (DEPTH, 2, 2, C_HEADS), 0.1),
        "mlstm_norm_g": gain(ks[13], (DEPTH, C_HEADS * C_DV)),
        "mem_norm_g": gain(ks[14], (DEPTH, D_MODEL)),
        "w_mem_kv": nrm(ks[15], (DEPTH, D_MODEL, 2 * D_W), D_MODEL ** -0.5),
        "w_proj_a": nrm(ks[16], (DEPTH, A_HEADS * A_VDIM, D_MODEL), (A_HEADS * A_VDIM) ** -0.5),
        "w_proj_b": nrm(ks[17], (DEPTH, B_W, D_MODEL), B_W ** -0.5),
        "w_proj_c": nrm(ks[18], (DEPTH, C_HEADS * C_DV, D_MODEL), (C_HEADS * C_DV) ** -0.5),
        "w_proj_d": nrm(ks[19], (DEPTH, D_W, D_MODEL), D_W ** -0.5),
        "w_out": nrm(ks[20], (DEPTH, D_MODEL, D_MODEL), D_MODEL ** -0.5),
        "final_norm_g": gain(ks[21], (D_MODEL,)),
    }


def reference(x, mem, positions, norm_g, w_in, mla_cq_norm_g, w_uq, mla_ckv_norm_g, w_ukv,
              conv_w, conv_b, conv_ln_g, conv_ln_b, mlstm_gate_b, mlstm_norm_g,
              mem_norm_g, w_mem_kv, w_proj_a, w_proj_b, w_proj_c, w_proj_d, w_out, final_norm_g):
    B, S, _ = x.shape
    for l in range(DEPTH):
        h = rmsnorm(x, norm_g[l])
        proj = h @ w_in[l]
        (a_cq, a_ckv, a_kr, a_gate, b_glu, b_gate, c_q, c_k, c_v, c_o, c_gate, c_if,
         d_q, merge) = jnp.split(proj, list(IN_SPLITS), axis=-1)
        y_a = mla_attention(a_cq, a_ckv, a_kr, positions, mla_cq_norm_g[l], w_uq[l],
                            mla_ckv_norm_g[l], w_ukv[l]) * jax.nn.silu(a_gate)
        y_b = conformer_conv(b_glu, conv_w[l], conv_b[l], conv_ln_g[l], conv_ln_b[l]) * jax.nn.silu(b_gate)
        y_c = mlstm_branch(c_q, c_k, c_v, c_o, c_if, mlstm_gate_b[l], mlstm_norm_g[l]) * jax.nn.silu(c_gate)
        y_d = memory_attention(d_q, mem, mem_norm_g[l], w_mem_kv[l])
        gates = jax.nn.sigmoid(merge.astype(jnp.float32)).astype(x.dtype).reshape(B, S, N_BRANCH, D_MODEL)
        z = (gates[:, :, 0] * (y_a @ w_proj_a[l])
             + gates[:, :, 1] * (y_b @ w_proj_b[l])
             + gates[:, :, 2] * (y_c @ w_proj_c[l])
             + gates[:, :, 3] * (y_d @ w_proj_d[l]))
        x = x + z @ w_out[l]
    return rmsnorm(x, final_norm_g)
```

```python
import math
from contextlib import ExitStack
import numpy as np
import ml_dtypes
import concourse.bass as bass
import concourse.mybir as mybir
from concourse.bass_utils import run_bass_kernel_spmd

F32 = mybir.dt.float32
BF16 = mybir.dt.bfloat16
I32 = mybir.dt.int32
AF = mybir.ActivationFunctionType
ALU = mybir.AluOpType
NCORES = 8


class Cfg:
    def __init__(s, D=4096, SEQ=16384, DEPTH=2, N_MEM=256):
        s.D = D; s.SEQ = SEQ; s.DEPTH = DEPTH; s.N_MEM = N_MEM
        s.BW = 2048; s.AH = 16; s.QR = 1024; s.KVR = 512; s.ROPE = 64
        s.CH = 4; s.DK = 256; s.DV = 512; s.DW = 512; s.CONVW = 31
        s.TOK = SEQ // NCORES
        s.NT = min(512, s.TOK)
        s.KC = D // 128
        sizes = [s.QR, s.KVR, s.ROPE, s.BW, 2 * s.BW, s.BW, s.CH * s.DK, s.CH * s.DK, s.CH * s.DV,
                 s.CH * s.DV, s.CH * s.DV, 4 * s.CH, s.DW, 4 * D]
        names = ["cq", "ckv", "kr", "agate", "bglu", "bgate", "mq", "mk", "mv", "mo", "mgate", "mif", "dq", "merge"]
        s.off = {}
        o = 0
        for n, z in zip(names, sizes):
            s.off[n] = (o, z); o += z
        s.INC = o


class DSem:
    __slots__ = ("sem", "cnt")

    def __init__(s, sem):
        s.sem = sem; s.cnt = 0


class Dep:
    __slots__ = ("w", "r", "sem", "name")
    ALL = []

    def __init__(s, name=""):
        s.w = {}; s.r = {}; s.sem = None; s.name = name
        Dep.ALL.append(s)


class Prog:
    ENG = ("pe", "act", "dve", "pool", "sp")

    def __init__(s, nc, stack):
        s.nc = nc; s.stack = stack
        s.items = {e: [] for e in s.ENG}
        s.esem = {e: stack.enter_context(nc.semaphore("esem_" + e)) for e in s.ENG}
        s.ecnt = {e: 0 for e in s.ENG}
        s.seen = {}
        s.dma_deps = []
        s.free_ds = []
        s.live = []
        s.nsb = 0

    def _collect(s, eng, reads, writes):
        need = {}
        def add(evs):
            for k, (sem, val, kind) in evs.items():
                if kind == "dma":
                    val = val.cnt
                elif kind == eng and eng == "pe":
                    continue
                if s.seen.get((eng, k), -1) >= val:
                    continue
                if k not in need or need[k][1] < val:
                    need[k] = (sem, val)
        for d in reads:
            add(d.w)
        for d in writes:
            add(d.w); add(d.r)
        for k, (sem, val) in need.items():
            s.seen[(eng, k)] = val
        return list(need.values())

    def _record(s, ev, reads, writes):
        k = id(ev[0])
        for d in reads:
            d.r[k] = ev
        for d in writes:
            d.w = {k: ev}; d.r = {}

    def op(s, eng, fn, reads=(), writes=(), inc=True):
        waits = s._collect(eng, reads, writes)
        sem = s.esem[eng]
        ev = (sem, s.ecnt[eng] + 1, eng)
        if inc:
            s.ecnt[eng] += 1
        s._record(ev, reads, writes)
        s.items[eng].append((waits, fn, (sem, 1) if inc else None))

    def _dsem(s, dep):
        if dep.sem is None:
            if s.free_ds:
                dep.sem = s.free_ds.pop()
            else:
                dep.sem = DSem(s.stack.enter_context(s.nc.semaphore("dsem%d" % len(s.dma_deps))))
                s.dma_deps.append(dep.sem)
            s.live.append(dep)
        return dep.sem

    def dma(s, eng, out, in_, reads, writes, semdep):
        waits = s._collect(eng, reads, writes)
        ds = s._dsem(semdep)
        ds.cnt += 16
        ev = (ds.sem, ds, "dma")
        s._record(ev, reads, writes)
        s.items[eng].append((waits, lambda e: e.dma_start(out=out, in_=in_), (ds.sem, 16)))

    def cc(s, ins, outs, reads, writes, semdep):
        waits = s._collect("pool", reads, writes)
        ds = s._dsem(semdep)
        ds.cnt += 1
        ev = (ds.sem, ds, "dma")
        s._record(ev, reads, writes)
        fn = lambda e: e.collective_compute("AllGather", ALU.bypass, replica_groups=[list(range(NCORES))],
                                            ins=ins, outs=outs)
        s.items["pool"].append((waits, fn, (ds.sem, 1)))

    def wait_all(s, eng, deps):
        waits = s._collect(eng, [], deps)
        s.items[eng].append((waits, None, None))

    def emit(s):
        nc = s.nc
        with nc.Block() as block:
            def run(e, items):
                for waits, fn, inc in items:
                    for sem, val in waits:
                        e.wait_ge(sem, val)
                    if fn is None:
                        continue
                    ins = fn(e)
                    if inc is not None:
                        ins.then_inc(inc[0], inc[1])

            @block.tensor
            def _(e):
                run(e, s.items["pe"])

            @block.scalar
            def _(e):
                run(e, s.items["act"])

            @block.vector
            def _(e):
                run(e, s.items["dve"])

            @block.gpsimd
            def _(e):
                run(e, s.items["pool"])

            @block.sync
            def _(e):
                run(e, s.items["sp"])

    def sb(s, stack, shape, dt, name=None):
        s.nsb += 1
        return stack.enter_context(s.nc.sbuf_tensor("%s_%d" % (name or "sb", s.nsb), list(shape), dt))


class Ring:
    def __init__(s, P, stack, n, shape, dt, name):
        s.bufs = [(P.sb(stack, shape, dt, name), Dep(name)) for _ in range(n)]
        s.i = 0

    def next(s):
        b = s.bufs[s.i % len(s.bufs)]; s.i += 1
        return b


class Builder:
    def __init__(s, cfg, debug=()):
        s.c = cfg; s.debug = set(debug)
        s.nc = bass.Bass("TRN2", target_bir_lowering=False)
        s.stack = ExitStack()
        s.P = Prog(s.nc, s.stack)
        s.dd = {}
        s.outs = []
        s.banks = [(s.stack.enter_context(s.nc.psum_tensor("bank%d" % i, [128, 512], F32)), Dep("bank%d" % i))
                   for i in range(8)]
        s.bi = 0

    def bank(s):
        b = s.banks[s.bi % 8]; s.bi += 1
        return b

    def din(s, name, shape, dt=F32):
        t = s.nc.dram_tensor(name, list(shape), dt, kind="ExternalInput")
        s.dd[name] = Dep(name)
        return t

    def dint(s, name, shape, dt, shared=False):
        kw = {"addr_space": "Shared"} if shared else {}
        t = s.nc.dram_tensor(name, list(shape), dt, **kw)
        s.dd[name] = Dep(name)
        return t

    def dout(s, name, shape, dt=F32):
        t = s.nc.dram_tensor(name, list(shape), dt, kind="ExternalOutput")
        s.dd[name] = Dep(name)
        s.outs.append(name)
        return t

    def barrier(s):
        P = s.P
        for e in P.ENG:
            waits = []
            for o in P.ENG:
                if o != e and o != "sp" and P.ecnt[o] > P.seen.get((e, id(P.esem[o])), -1):
                    waits.append((P.esem[o], P.ecnt[o])); P.seen[(e, id(P.esem[o]))] = P.ecnt[o]
            for d in P.dma_deps:
                if d.cnt > P.seen.get((e, id(d.sem)), -1):
                    waits.append((d.sem, d.cnt)); P.seen[(e, id(d.sem))] = d.cnt
            P.items[e].append((waits, None, None))
        for dep in P.live:
            dep.sem = None
        P.live = []
        P.free_ds = list(P.dma_deps)
        for dep in Dep.ALL:
            dep.w = {}; dep.r = {}

    def allgather(s, src, src_dep, dst, dst_dep, dst_ap=None):
        P = s.P
        P.cc([src.ap()], [dst.ap() if dst_ap is None else dst_ap], [src_dep], [dst_dep], dst_dep)
        if True:
            return
        if not hasattr(s, "fsrc"):
            s.fsrc = s.dint("fence_src", [1, 64], F32); s.fdst = s.dint("fence_dst", [NCORES, 64], F32, shared=True)
        P.cc([s.fsrc.ap()], [s.fdst.ap()], [], [dst_dep, s.dd["fence_dst"]], s.dd["fence_dst"])

    def linear(s, st, W, K, cols, Xsrc, T, epi, wblk=1024):
        P = s.P; NT = s.c.NT; KC = K // 128
        with ExitStack() as ls:
            KH = min(KC, 4)
            wst = Ring(P, ls, 2, [128, KH, wblk], F32, "wst")
            wbf = P.sb(ls, [128, KC, wblk], BF16, "wbf"); wbf_d = Dep("wbf")
            xr = Ring(P, ls, 2, [128, KC, NT], BF16, "xr")
            blocks = []; cur = []; used = 0
            for g in cols:
                if used + g[1] > wblk:
                    blocks.append(cur); cur = []; used = 0
                cur.append((g, used)); used += g[1]
            if cur:
                blocks.append(cur)
            for blk in blocks:
                for (c0, m, tag), so in blk:
                    for k0 in range(0, KC, KH):
                        wt, wd = wst.next()
                        P.dma("sp", wt[:, 0:KH, 0:m],
                              W(k0 * 128, (k0 + KH) * 128, c0, c0 + m).rearrange("(kc p) n -> p kc n", p=128),
                              [], [wd], wd)
                        P.op("pool", (lambda wt=wt, k0=k0, so=so, m=m: lambda e: e.tensor_copy(
                            out=wbf[:, k0:k0 + KH, so:so + m], in_=wt[:, 0:KH, 0:m]))(), [wd], [wbf_d])
                for tt in range(T // NT):
                    xt, xd = xr.next()
                    xap, xdep = Xsrc(tt)
                    P.dma("sp", xt[:], xap.rearrange("(kc p) n -> p kc n", p=128), [xdep], [xd], xd)
                    for (c0, m, tag), so in blk:
                        bk, bd = s.bank()
                        for kc in range(KC):
                            P.op("pe", (lambda bk=bk, kc=kc, so=so, m=m, xt=xt: lambda e: e.matmul(
                                bk[0:m, 0:NT], lhsT=wbf[:, kc, so:so + m], rhs=xt[:, kc, :],
                                start=(kc == 0), stop=(kc == KC - 1)))(),
                                [wbf_d, xd], [bd], inc=(kc == KC - 1))
                        epi(tag, bk, bd, m, tt)

    def fm_norm(s, src, src_dep, F, gain, dst, dst_dep, T, nt=None):
        P = s.P; NT = nt or s.c.NT; FC = F // 128
        with ExitStack() as ps:
            xs = Ring(P, ps, 2, [128, FC, NT], F32, "nx"); sq = Ring(P, ps, 2, [128, FC, NT], F32, "nq")
            rs = Ring(P, ps, 2, [128, NT], F32, "nr"); hs = Ring(P, ps, 2, [128, FC, NT], BF16, "nh")
            for tt in range(T // NT):
                tsl = slice(tt * NT, (tt + 1) * NT)
                xt, xd = xs.next(); qt, qd = sq.next(); rt, rd = rs.next(); ht, hd = hs.next()
                P.dma("sp", xt[:], src.ap()[:, tsl].rearrange("(kc p) n -> p kc n", p=128), [src_dep], [xd], xd)
                P.op("act", (lambda xt=xt, qt=qt: lambda e: e.activation(out=qt[:], in_=xt[:], func=AF.Square))(), [xd], [qd])
                bk, bd = s.bank()
                for kc in range(FC):
                    P.op("pe", (lambda bk=bk, qt=qt, kc=kc: lambda e: e.matmul(
                        bk[:, 0:NT], lhsT=s.ones_f[:], rhs=qt[:, kc, :], start=(kc == 0), stop=(kc == FC - 1)))(),
                        [s.ones_d, qd], [bd], inc=(kc == FC - 1))
                P.op("act", (lambda bk=bk, rt=rt: lambda e: e.activation(out=rt[:], in_=bk[:, 0:NT], func=AF.Sqrt,
                                                                         bias=1e-6, scale=1.0 / F))(), [bd], [rd])
                P.op("dve", (lambda rt=rt: lambda e: e.reciprocal(out=rt[:], in_=rt[:]))(), [rd], [rd])
                for kc in range(FC):
                    P.op("dve", (lambda xt=xt, ht=ht, rt=rt, kc=kc: lambda e: e.scalar_tensor_tensor(
                        out=ht[:, kc, :], in0=xt[:, kc, :], scalar=gain(kc), in1=rt[:],
                        op0=ALU.mult, op1=ALU.mult))(), [xd, rd, s.const_d], [hd])
                P.dma("sp", dst.ap()[:, tsl].rearrange("(kc p) n -> p kc n", p=128), ht[:], [hd], [dst_dep], hd)

    def rope_tables(s):
        P = s.P; NT = s.c.NT; T = s.c.SEQ
        s.cosT = s.dint("cosT", [64, T], F32); s.sinT = s.dint("sinT", [64, T], F32)
        MAG = 12582912.0; C1 = 6.28125; C2 = 2.0 * math.pi - 6.28125
        with ExitStack() as ps:
            pi_ = Ring(P, ps, 2, [64, NT], I32, "rpi"); a_ = Ring(P, ps, 2, [64, NT], F32, "ra")
            k_ = Ring(P, ps, 2, [64, NT], F32, "rk"); c_ = Ring(P, ps, 2, [64, NT], F32, "rc"); s_ = Ring(P, ps, 2, [64, NT], F32, "rs")
            for tt in range(T // NT):
                tsl = slice(tt * NT, (tt + 1) * NT)
                pt, pd = pi_.next(); at, ad = a_.next(); kt, kd = k_.next(); ct, cd = c_.next(); st_, sd = s_.next()
                P.dma("sp", pt[:], bass.AP(s.pos, tt * NT, [[0, 64], [1, NT]]), [], [pd], pd)
                P.op("dve", (lambda at=at, pt=pt: lambda e: e.tensor_copy(out=at[:], in_=pt[:]))(), [pd], [ad])
                P.op("dve", (lambda at=at: lambda e: e.tensor_scalar(out=at[:], in0=at[:], scalar1=s.invf_sb[:, 0:1], scalar2=None,
                                                                     op0=ALU.mult))(), [ad, s.const_d], [ad])
                P.op("dve", (lambda at=at, kt=kt: lambda e: e.tensor_scalar(out=kt[:], in0=at[:], scalar1=1.0 / (2 * math.pi), scalar2=MAG,
                                                                            op0=ALU.mult, op1=ALU.add))(), [ad], [kd])
                P.op("dve", (lambda kt=kt: lambda e: e.tensor_scalar(out=kt[:], in0=kt[:], scalar1=MAG, scalar2=None,
                                                                     op0=ALU.subtract))(), [kd], [kd])
                P.op("dve", (lambda kt=kt, at=at: lambda e: e.scalar_tensor_tensor(out=at[:], in0=kt[:], scalar=-C1, in1=at[:],
                                                                                  op0=ALU.mult, op1=ALU.add))(), [kd, ad], [ad])
                P.op("dve", (lambda kt=kt, at=at: lambda e: e.scalar_tensor_tensor(out=at[:], in0=kt[:], scalar=-C2, in1=at[:],
                                                                                  op0=ALU.mult, op1=ALU.add))(), [kd, ad], [ad])
                P.op("act", (lambda at=at, st_=st_: lambda e: e.activation(out=st_[:], in_=at[:], func=AF.Sin))(), [ad], [sd])
                P.op("act", (lambda kt=kt, at=at: lambda e: e.activation(out=kt[:], in_=at[:], func=AF.Abs))(), [ad], [kd])
                P.op("act", (lambda kt=kt, ct=ct: lambda e: e.activation(out=ct[:], in_=kt[:], func=AF.Sin, scale=-1.0,
                                                                         bias=s.halfpi[:, 0:1]))(), [kd, s.const_d], [cd])
                P.dma("sp", s.cosT.ap()[:, tsl], ct[:], [cd], [s.dd["cosT"]], cd)
                P.dma("sp", s.sinT.ap()[:, tsl], st_[:], [sd], [s.dd["sinT"]], sd)

    def rope_apply(s, ps, xt, xd, tt, out_t, out_d, rings):
        P = s.P; NT = s.c.NT
        tsl = slice(tt * NT, (tt + 1) * NT)
        (ct, cd), (st_, sd), (t1, t1d) = rings[0].next(), rings[1].next(), rings[2].next()
        P.dma("sp", ct[:], s.cosT.ap()[:, tsl], [s.dd["cosT"]], [cd], cd)
        P.dma("sp", st_[:], s.sinT.ap()[:, tsl], [s.dd["sinT"]], [sd], sd)
        bk, bd = s.bank()
        P.op("pe", lambda e: e.matmul(bk[0:64, 0:NT], lhsT=s.rot_sb[:], rhs=xt[0:64, :], start=True, stop=True), [s.const_d, xd], [bd])
        P.op("dve", lambda e: e.tensor_tensor(out=t1[:], in0=xt[0:64, :], in1=ct[:], op=ALU.mult), [xd, cd], [t1d])
        P.op("dve", lambda e: e.tensor_tensor(out=st_[:], in0=bk[0:64, 0:NT], in1=st_[:], op=ALU.mult), [bd, sd], [sd])
        P.op("dve", lambda e: e.tensor_tensor(out=out_t[0:64, :], in0=t1[:], in1=st_[:], op=ALU.add), [t1d, sd], [out_d])

    def phase_attn(s, l):
        c = s.c; P = s.P; NT = c.NT; T = c.SEQ; NTT = T // NT; NKT = T // 128
        dd = s.dd
        s.fm_norm(s.cq_g, dd["cq_g"], c.QR, lambda kc: s.cqg_sb[:, l, kc:kc + 1], s.cqn, dd["cqn"], T)
        s.fm_norm(s.ckv_g, dd["ckv_g"], c.KVR, lambda kc: s.ckvg_sb[:, l, kc:kc + 1], s.ckvn, dd["ckvn"], T)
        s.barrier()
        with ExitStack() as ps:
            ogb = Ring(P, ps, 4, [128, NT], BF16, "aqo"); xq = Ring(P, ps, 2, [64, NT], F32, "axq")
            rr = [Ring(P, ps, 2, [64, NT], F32, "rc1"), Ring(P, ps, 2, [64, NT], F32, "rc2"), Ring(P, ps, 2, [64, NT], F32, "rc3")]

            def epi_q(tag, bk, bd, m, tt):
                kind, h = tag
                tsl = slice(tt * NT, (tt + 1) * NT)
                t_, d_ = ogb.next()
                if kind == "n":
                    P.op("act", lambda e: e.activation(out=t_[0:128, :], in_=bk[0:128, 0:NT], func=AF.Copy), [bd], [d_])
                    P.dma("act", s.qn.ap()[h * 128:(h + 1) * 128, tsl], t_[0:128, :], [d_], [dd["qn"]], d_)
                else:
                    x_, xd_ = xq.next()
                    P.op("act", lambda e: e.activation(out=x_[:], in_=bk[0:64, 0:NT], func=AF.Copy), [bd], [xd_])
                    s.rope_apply(ps, x_, xd_, tt, t_, d_, rr)
                    P.dma("sp", s.qr.ap()[h * 64:(h + 1) * 64, tsl], t_[0:64, :], [d_], [dd["qr"]], d_)
            cols = [(0, 128, ("n", 0)), (128, 64, ("r", 0)), (192, 128, ("n", 1)), (320, 64, ("r", 1))]
            s.linear(ps, lambda k0, k1, c0, c1: s.wuq.ap()[l, k0:k1, c0:c1], c.QR, cols,
                     lambda tt: (s.cqn.ap()[:, tt * NT:(tt + 1) * NT], dd["cqn"]), T, epi_q, wblk=384)

            def epi_k(tag, bk, bd, m, tt):
                h = tag
                tsl = slice(tt * NT, (tt + 1) * NT)
                t_, d_ = ogb.next()
                P.op("act", lambda e: e.activation(out=t_[0:128, :], in_=bk[0:128, 0:NT], func=AF.Copy), [bd], [d_])
                P.dma("act", s.kn.ap()[h * 128:(h + 1) * 128, tsl], t_[0:128, :], [d_], [dd["kn"]], d_)
            s.linear(ps, lambda k0, k1, c0, c1: s.wukv.ap()[l, k0:k1, c0:c1], c.KVR, [(0, 128, 0), (256, 128, 1)],
                     lambda tt: (s.ckvn.ap()[:, tt * NT:(tt + 1) * NT], dd["ckvn"]), T, epi_k, wblk=128)
            kx = Ring(P, ps, 2, [64, NT], F32, "kx")
            for tt in range(NTT):
                tsl = slice(tt * NT, (tt + 1) * NT)
                x_, xd_ = kx.next(); t_, d_ = ogb.next()
                P.dma("sp", x_[:], s.kr_g.ap()[:, tsl], [dd["kr_g"]], [xd_], xd_)
                s.rope_apply(ps, x_, xd_, tt, t_, d_, rr)
                P.dma("sp", s.krp.ap()[:, tsl], t_[0:64, :], [d_], [dd["krp"]], d_)
        s.barrier()
        KVC = c.KVR // 128
        with ExitStack() as ps:
            wv_f = P.sb(ps, [128, KVC, 256], F32, "wvf"); wv_b = P.sb(ps, [128, KVC, 256], BF16, "wvb"); wv_d = Dep("wv")
            for h in range(2):
                P.dma("sp", wv_f[:, :, h * 128:(h + 1) * 128],
                      s.wukv.ap()[l, :, h * 256 + 128:h * 256 + 256].rearrange("(kc p) n -> p kc n", p=128), [], [wv_d], wv_d)
            P.op("pool", lambda e: e.tensor_copy(out=wv_b[:], in_=wv_f[:]), [wv_d], [wv_d])
            xr = Ring(P, ps, 2, [128, KVC, NT], BF16, "vx"); vo = Ring(P, ps, 2, [128, NT // 128, 256], BF16, "vo")
            for tt in range(NTT):
                tsl = slice(tt * NT, (tt + 1) * NT)
                xt, xd = xr.next(); ot, od = vo.next()
                P.dma("sp", xt[:], s.ckvn.ap()[:, tsl].rearrange("(kc p) n -> p kc n", p=128), [dd["ckvn"]], [xd], xd)
                for j in range(NT // 128):
                    bk, bd = s.bank()
                    for kc in range(KVC):
                        P.op("pe", (lambda bk=bk, kc=kc, j=j, xt=xt: lambda e: e.matmul(
                            bk[:, 0:256], lhsT=xt[:, kc, j * 128:(j + 1) * 128], rhs=wv_b[:, kc, :],
                            start=(kc == 0), stop=(kc == KVC - 1)))(), [xd, wv_d], [bd], inc=(kc == KVC - 1))
                    P.op("act", (lambda bk=bk, ot=ot, j=j: lambda e: e.activation(out=ot[:, j, :], in_=bk[:, 0:256], func=AF.Copy))(), [bd], [od])
                P.dma("sp", s.vtok.ap()[tsl, :].rearrange("(j p) n -> p j n", p=128), ot[:], [od], [dd["vtok"]], od)
        s.barrier()
        scale = (128 + 64) ** -0.5
        with ExitStack() as ps:
            knb = P.sb(ps, [128, T], BF16, "knb"); krb = P.sb(ps, [64, T], BF16, "krb"); vb = P.sb(ps, [128, NKT, 128], BF16, "vb")
            kv_d = Dep("kv")
            qnb = Ring(P, ps, 2, [128, NT], BF16, "qnb"); qrb = Ring(P, ps, 2, [64, NT], BF16, "qrb")
            pt_ = Ring(P, ps, 3, [128, NT], BF16, "pt"); rc = Ring(P, ps, 2, [128, NT], F32, "arc")
            ob = Ring(P, ps, 2, [128, NT], BF16, "aob"); gt_ = Ring(P, ps, 2, [128, NT], BF16, "agt")
            qi = 0
            for h in range(2):
                P.dma("sp", knb[:], s.kn.ap()[h * 128:(h + 1) * 128, :], [dd["kn"]], [kv_d], kv_d)
                P.dma("sp", krb[:], s.krp.ap(), [dd["krp"]], [kv_d], kv_d)
                P.dma("sp", vb[:], s.vtok.ap()[:, h * 128:(h + 1) * 128].rearrange("(j p) n -> p j n", p=128), [dd["vtok"]], [kv_d], kv_d)
                for tt in range(NTT):
                    tsl = slice(tt * NT, (tt + 1) * NT)
                    qn_, qnd = qnb.next(); qr_, qrd = qrb.next()
                    P.dma("sp", qn_[:], s.qn.ap()[h * 128:(h + 1) * 128, tsl], [dd["qn"]], [qnd], qnd)
                    P.dma("sp", qr_[:], s.qr.ap()[h * 64:(h + 1) * 64, tsl], [dd["qr"]], [qrd], qrd)
                    obk, obd = s.banks[4 + 2 * (qi % 2)]; dbk, dbd = s.banks[5 + 2 * (qi % 2)]; qi += 1
                    for kt in range(NKT):
                        sbk, sbd = s.banks[kt % 4]
                        ksl = slice(kt * 128, (kt + 1) * 128)
                        P.op("pe", (lambda sbk=sbk, ksl=ksl, qn_=qn_: lambda e: e.matmul(
                            sbk[:, 0:NT], lhsT=knb[:, ksl], rhs=qn_[:], start=True, stop=False))(), [kv_d, qnd], [sbd], inc=False)
                        P.op("pe", (lambda sbk=sbk, ksl=ksl, qr_=qr_: lambda e: e.matmul(
                            sbk[:, 0:NT], lhsT=krb[:, ksl], rhs=qr_[:], start=False, stop=True))(), [kv_d, qrd], [sbd])
                        p_, pd = pt_.next()
                        P.op("act", (lambda sbk=sbk, p_=p_: lambda e: e.activation(out=p_[:], in_=sbk[:, 0:NT], func=AF.Exp, scale=scale))(), [sbd], [pd])
                        last = (kt == NKT - 1)
                        P.op("pe", (lambda obk=obk, p_=p_, kt=kt, last=last: lambda e: e.matmul(
                            obk[:, 0:NT], lhsT=vb[:, kt, :], rhs=p_[:], start=(kt == 0), stop=last))(), [kv_d, pd], [obd], inc=last)
                        P.op("pe", (lambda dbk=dbk, p_=p_, kt=kt, last=last: lambda e: e.matmul(
                            dbk[:, 0:NT], lhsT=s.ones_b[:], rhs=p_[:], start=(kt == 0), stop=last))(), [s.const_d, pd], [dbd], inc=True)
                    r_, rd = rc.next(); o_, od = ob.next(); g_, gd = gt_.next()
                    P.dma("sp", g_[:], s.inter["agate"].ap()[h * 128:(h + 1) * 128, tsl], [dd["p1_agate"]], [gd], gd)
                    P.op("dve", (lambda r_=r_, dbk=dbk: lambda e: e.reciprocal(out=r_[:], in_=dbk[:, 0:NT]))(), [dbd], [rd])
                    P.op("dve", (lambda r_=r_, obk=obk: lambda e: e.tensor_tensor(out=r_[:], in0=obk[:, 0:NT], in1=r_[:], op=ALU.mult))(), [obd, rd], [rd])
                    P.op("dve", (lambda r_=r_, o_=o_, g_=g_: lambda e: e.tensor_tensor(out=o_[:], in0=r_[:], in1=g_[:], op=ALU.mult))(), [rd, gd], [od])
                    P.dma("sp", s.ya_b.ap()[h * 128:(h + 1) * 128, tsl], o_[:], [od], [dd["ya_b"]], od)

    def phase_conv(s, l):
        c = s.c; P = s.P; NT = c.NT; T = c.SEQ; NTT = T // NT; dd = s.dd; HW = c.CONVW // 2
        with ExitStack() as ps:
            ub = P.sb(ps, [128, T + 2 * HW], BF16, "ub"); ub_d = Dep("ub")
            ga = P.sb(ps, [128, T], BF16, "ga"); ga_d = Dep("ga")
            acc = Ring(P, ps, 2, [128, NT], F32, "cacc")
            for ct in range(2):
                P.dma("sp", ub[:, HW:HW + T], s.inter["glua"].ap()[ct * 128:(ct + 1) * 128, :], [dd["p1_glua"]], [ub_d], ub_d)
                P.dma("sp", ga[:], s.inter["glug"].ap()[ct * 128:(ct + 1) * 128, :], [dd["p1_glug"]], [ga_d], ga_d)
                P.op("pool", lambda e: e.memset(ub[:, 0:HW], 0.0), [], [ub_d])
                P.op("pool", lambda e: e.memset(ub[:, HW + T:HW + T + HW], 0.0), [], [ub_d])
                P.op("pool", lambda e: e.tensor_tensor(out=ub[:, HW:HW + T], in0=ub[:, HW:HW + T], in1=ga[:], op=ALU.mult), [ub_d, ga_d], [ub_d])
                for tt in range(NTT):
                    a_, ad = acc.next()
                    t0 = tt * NT
                    P.op("dve", (lambda a_=a_, t0=t0, ct=ct: lambda e: e.tensor_scalar(
                        out=a_[:], in0=ub[:, t0:t0 + NT], scalar1=s.cw_sb[:, l, ct, 0:1], scalar2=s.cb_sb[:, l, ct:ct + 1],
                        op0=ALU.mult, op1=ALU.add))(), [ub_d, s.const_d], [ad])
                    for k in range(1, c.CONVW):
                        P.op("dve", (lambda a_=a_, t0=t0, ct=ct, k=k: lambda e: e.scalar_tensor_tensor(
                            out=a_[:], in0=ub[:, t0 + k:t0 + k + NT], scalar=s.cw_sb[:, l, ct, k:k + 1], in1=a_[:],
                            op0=ALU.mult, op1=ALU.add))(), [ub_d, ad, s.const_d], [ad])
                    P.dma("sp", s.cvo.ap()[ct * 128:(ct + 1) * 128, t0:t0 + NT], a_[:], [ad], [dd["cvo"]], ad)
        s.barrier()
        with ExitStack() as ps:
            xs = Ring(P, ps, 2, [128, 2, NT], F32, "lx"); sq = Ring(P, ps, 2, [128, 2, NT], F32, "lq")
            row = Ring(P, ps, 2, [1, 2, NT], F32, "lrow")
            for tt in range(NTT):
                tsl = slice(tt * NT, (tt + 1) * NT)
                xt, xd = xs.next(); qt, qd = sq.next(); rt, rd = row.next()
                P.dma("sp", xt[:], s.cvo.ap()[:, tsl].rearrange("(kc p) n -> p kc n", p=128), [dd["cvo"]], [xd], xd)
                P.op("act", (lambda xt=xt, qt=qt: lambda e: e.activation(out=qt[:], in_=xt[:], func=AF.Square))(), [xd], [qd])
                b1, b1d = s.bank(); b2, b2d = s.bank()
                for kc in range(2):
                    P.op("pe", (lambda b1=b1, xt=xt, kc=kc: lambda e: e.matmul(b1[:, 0:NT], lhsT=s.ones_f[:], rhs=xt[:, kc, :],
                                                                               start=(kc == 0), stop=(kc == 1)))(), [s.ones_d, xd], [b1d], inc=(kc == 1))
                for kc in range(2):
                    P.op("pe", (lambda b2=b2, qt=qt, kc=kc: lambda e: e.matmul(b2[:, 0:NT], lhsT=s.ones_f[:], rhs=qt[:, kc, :],
                                                                               start=(kc == 0), stop=(kc == 1)))(), [s.ones_d, qd], [b2d], inc=(kc == 1))
                P.op("dve", (lambda rt=rt, b1=b1: lambda e: e.tensor_copy(out=rt[:, 0, :], in_=b1[0:1, 0:NT]))(), [b1d], [rd])
                P.op("dve", (lambda rt=rt, b2=b2: lambda e: e.tensor_copy(out=rt[:, 1, :], in_=b2[0:1, 0:NT]))(), [b2d], [rd])
                P.dma("sp", s.lnp.ap()[0:1, tsl], rt[:, 0, :], [rd], [dd["lnp"]], rd)
                P.dma("sp", s.lnp.ap()[1:2, tsl], rt[:, 1, :], [rd], [dd["lnp"]], rd)
            s.allgather(s.lnp, dd["lnp"], s.lng_, dd["lng_"])
            g16 = Ring(P, ps, 2, [2 * NCORES, NT], F32, "g16")
            mu = Ring(P, ps, 2, [128, NT], F32, "lmu"); rs = Ring(P, ps, 2, [128, NT], F32, "lrs")
            yo = Ring(P, ps, 2, [128, NT], F32, "lyo"); yb = Ring(P, ps, 2, [128, NT], BF16, "lyb"); gt_ = Ring(P, ps, 2, [128, NT], BF16, "lgt")
            BWf = float(c.BW)
            for tt in range(NTT):
                tsl = slice(tt * NT, (tt + 1) * NT)
                xt, xd = xs.next(); gt, gd = g16.next(); m_, md = mu.next(); r_, rd = rs.next()
                P.dma("sp", gt[:], s.lng_.ap()[:, tsl], [dd["lng_"]], [gd], gd)
                P.dma("sp", xt[:], s.cvo.ap()[:, tsl].rearrange("(kc p) n -> p kc n", p=128), [dd["cvo"]], [xd], xd)
                b1, b1d = s.bank(); b2, b2d = s.bank()
                P.op("pe", (lambda b1=b1, gt=gt: lambda e: e.matmul(b1[:, 0:NT], lhsT=s.sel1[:], rhs=gt[:], start=True, stop=True))(), [s.const_d, gd], [b1d])
                P.op("pe", (lambda b2=b2, gt=gt: lambda e: e.matmul(b2[:, 0:NT], lhsT=s.sel2[:], rhs=gt[:], start=True, stop=True))(), [s.const_d, gd], [b2d])
                P.op("act", (lambda m_=m_, b1=b1: lambda e: e.activation(out=m_[:], in_=b1[:, 0:NT], func=AF.Copy, scale=1.0 / BWf))(), [b1d], [md])
                P.op("dve", (lambda r_=r_, m_=m_: lambda e: e.tensor_tensor(out=r_[:], in0=m_[:], in1=m_[:], op=ALU.mult))(), [md], [rd])
                P.op("dve", (lambda r_=r_, b2=b2: lambda e: e.scalar_tensor_tensor(out=r_[:], in0=b2[:, 0:NT], scalar=1.0 / BWf, in1=r_[:],
                                                                                  op0=ALU.mult, op1=ALU.subtract))(), [b2d, rd], [rd])
                P.op("act", (lambda r_=r_: lambda e: e.activation(out=r_[:], in_=r_[:], func=AF.Sqrt, bias=1e-6, scale=1.0))(), [rd], [rd])
                P.op("dve", (lambda r_=r_: lambda e: e.reciprocal(out=r_[:], in_=r_[:]))(), [rd], [rd])
                for ct in range(2):
                    y_, yd = yo.next(); o_, od = yb.next(); g_, gdd = gt_.next()
                    P.dma("sp", g_[:], s.inter["bgate"].ap()[ct * 128:(ct + 1) * 128, tsl], [dd["p1_bgate"]], [gdd], gdd)
                    P.op("dve", (lambda y_=y_, xt=xt, m_=m_, ct=ct: lambda e: e.tensor_tensor(out=y_[:], in0=xt[:, ct, :], in1=m_[:], op=ALU.subtract))(), [xd, md], [yd])
                    P.op("dve", (lambda y_=y_, r_=r_: lambda e: e.tensor_tensor(out=y_[:], in0=y_[:], in1=r_[:], op=ALU.mult))(), [yd, rd], [yd])
                    P.op("act", (lambda y_=y_, ct=ct: lambda e: e.activation(out=y_[:], in_=y_[:], func=AF.Silu, scale=s.lg_sb[:, l, ct:ct + 1],
                                                                             bias=s.lb_sb[:, l, ct:ct + 1]))(), [yd, s.const_d], [yd])
                    P.op("dve", (lambda y_=y_, o_=o_, g_=g_: lambda e: e.tensor_tensor(out=o_[:], in0=y_[:], in1=g_[:], op=ALU.mult))(), [yd, gdd], [od])
                    P.dma("sp", s.yb_b.ap()[ct * 128:(ct + 1) * 128, tsl], o_[:], [od], [dd["yb_b"]], od)

    def phase_mem(s, l):
        c = s.c; P = s.P; NT = c.NT; T = c.SEQ; NTT = T // NT; dd = s.dd; D = c.D; KC = c.KC; NM = c.N_MEM
        s.fm_norm(s.memT, dd["memT"], D, lambda kc: s.memg_sb[:, l, kc:kc + 1], s.memn, dd["memn"], NM, nt=NM)
        s.barrier()
        scale = 128 ** -0.5
        with ExitStack() as ps:
            mn = P.sb(ps, [128, KC, NM], BF16, "mn"); wf = Ring(P, ps, 2, [128, 4, 256], F32, "mwf")
            wb = P.sb(ps, [128, KC, 256], BF16, "mwb"); md = Dep("mn"); wd_ = Dep("mwb")
            kT = P.sb(ps, [128, NM], BF16, "mkT"); vm = P.sb(ps, [128, NM // 128, 128], BF16, "mvm"); kvd = Dep("mkv")
            P.dma("sp", mn[:], s.memn.ap().rearrange("(kc p) n -> p kc n", p=128), [dd["memn"]], [md], md)
            for k0 in range(0, KC, 4):
                wt, wtd = wf.next()
                P.dma("sp", wt[:], s.wmkv.ap()[l, k0 * 128:(k0 + 4) * 128, :].rearrange("(kc p) n -> p kc n", p=128), [], [wtd], wtd)
                P.op("pool", (lambda wt=wt, k0=k0: lambda e: e.tensor_copy(out=wb[:, k0:k0 + 4, :], in_=wt[:]))(), [wtd], [wd_])
            bk, bd = s.bank()
            for kc in range(KC):
                P.op("pe", (lambda kc=kc: lambda e: e.matmul(bk[:, 0:NM], lhsT=wb[:, kc, 0:128], rhs=mn[:, kc, :],
                                                              start=(kc == 0), stop=(kc == KC - 1)))(), [wd_, md], [bd], inc=(kc == KC - 1))
            P.op("act", lambda e: e.activation(out=kT[:], in_=bk[:, 0:NM], func=AF.Copy), [bd], [kvd])
            for j in range(NM // 128):
                bk2, bd2 = s.bank()
                for kc in range(KC):
                    P.op("pe", (lambda kc=kc, j=j, bk2=bk2: lambda e: e.matmul(bk2[:, 0:128], lhsT=mn[:, kc, j * 128:(j + 1) * 128], rhs=wb[:, kc, 128:256],
                                                                               start=(kc == 0), stop=(kc == KC - 1)))(), [wd_, md], [bd2], inc=(kc == KC - 1))
                P.op("act", (lambda j=j, bk2=bk2: lambda e: e.activation(out=vm[:, j, :], in_=bk2[:, 0:128], func=AF.Copy))(), [bd2], [kvd])
            qb = Ring(P, ps, 2, [128, NT], BF16, "mq"); pt_ = Ring(P, ps, 3, [128, NT], BF16, "mp")
            rc = Ring(P, ps, 2, [128, NT], F32, "mrc"); ob = Ring(P, ps, 2, [128, NT], BF16, "mob")
            for tt in range(NTT):
                tsl = slice(tt * NT, (tt + 1) * NT)
                q_, qd = qb.next()
                P.dma("sp", q_[:], s.inter["dq"].ap()[:, tsl], [dd["p1_dq"]], [qd], qd)
                obk, obd = s.banks[4 + 2 * (tt % 2)]; dbk, dbd = s.banks[5 + 2 * (tt % 2)]
                NJ = NM // 128
                for j in range(NJ):
                    sbk, sbd = s.banks[j % 4]
                    P.op("pe", (lambda sbk=sbk, j=j, q_=q_: lambda e: e.matmul(sbk[:, 0:NT], lhsT=kT[:, j * 128:(j + 1) * 128], rhs=q_[:], start=True, stop=True))(), [kvd, qd], [sbd])
                    p_, pd = pt_.next()
                    P.op("act", (lambda sbk=sbk, p_=p_: lambda e: e.activation(out=p_[:], in_=sbk[:, 0:NT], func=AF.Exp, scale=scale))(), [sbd], [pd])
                    last = (j == NJ - 1)
                    P.op("pe", (lambda obk=obk, p_=p_, j=j, last=last: lambda e: e.matmul(obk[:, 0:NT], lhsT=vm[:, j, :], rhs=p_[:], start=(j == 0), stop=last))(), [kvd, pd], [obd], inc=last)
                    P.op("pe", (lambda dbk=dbk, p_=p_, j=j, last=last: lambda e: e.matmul(dbk[:, 0:NT], lhsT=s.ones_b[:], rhs=p_[:], start=(j == 0), stop=last))(), [s.const_d, pd], [dbd], inc=True)
                r_, rd = rc.next(); o_, od = ob.next()
                P.op("dve", (lambda r_=r_, dbk=dbk: lambda e: e.reciprocal(out=r_[:], in_=dbk[:, 0:NT]))(), [dbd], [rd])
                P.op("dve", (lambda r_=r_, obk=obk, o_=o_: lambda e: e.tensor_tensor(out=o_[:], in0=obk[:, 0:NT], in1=r_[:], op=ALU.mult))(), [obd, rd], [od])
                P.dma("sp", s.yd_b.ap()[:, tsl], o_[:], [od], [dd["yd_b"]], od)

    def phase_out(s, l, xcur, xcur_d, xnext, xnext_d):
        c = s.c; P = s.P; NT = c.NT; T = c.SEQ; dd = s.dd; D = c.D; Dc = D // NCORES
        srcs = [(s.ya_b, "ya_b", 256, c.BW, s.wpa), (s.yb_b, "yb_b", 256, c.BW, s.wpb), (s.yc_b, "yc_b", 512, c.BW, s.wpc),
                (s.yd_b, "yd_b", 128, c.DW, s.wpd)]
        for br, (ysrc, ysn, yrows, Kb, wp) in enumerate(srcs):
            yt, yn = s.ybuf, "ybuf"
            s.allgather(ysrc, dd[ysn], s.ybuf, dd["ybuf"], dst_ap=s.ybuf.ap()[0:NCORES * yrows, :])
            with ExitStack() as ps:
                gt_ = Ring(P, ps, 2, [128, NT], BF16, "ggt"); za = Ring(P, ps, 2, [128, NT], F32, "gza")
                zo = Ring(P, ps, 2, [128, NT], F32, "gzo"); zb_ = Ring(P, ps, 2, [128, NT], BF16, "gzb")
                mg = s.inter["mg%d" % br]

                def epi(tag, bk, bd, m, tt, br=br, mg=mg):
                    j = tag
                    tsl = slice(tt * NT, (tt + 1) * NT)
                    g_, gd = gt_.next(); o_, od = zo.next()
                    P.dma("sp", g_[0:m, :], mg.ap()[j:j + m, tsl], [dd["p1_mg%d" % br]], [gd], gd)
                    P.op("dve", lambda e: e.tensor_tensor(out=o_[0:m, :], in0=bk[0:m, 0:NT], in1=g_[0:m, :], op=ALU.mult), [bd, gd], [od])
                    if br > 0:
                        a_, ad = za.next()
                        P.dma("sp", a_[0:m, :], s.zacc.ap()[j:j + m, tsl], [dd["zacc"]], [ad], ad)
                        P.op("dve", lambda e: e.tensor_tensor(out=o_[0:m, :], in0=o_[0:m, :], in1=a_[0:m, :], op=ALU.add), [od, ad], [od])
                    if br < 3:
                        P.dma("sp", s.zacc.ap()[j:j + m, tsl], o_[0:m, :], [od], [dd["zacc"]], od)
                    else:
                        b_, bdd = zb_.next()
                        P.op("act", lambda e: e.activation(out=b_[0:m, :], in_=o_[0:m, :], func=AF.Copy), [od], [bdd])
                        P.dma("sp", s.zb.ap()[j:j + m, tsl], b_[0:m, :], [bdd], [dd["zb"]], bdd)
                cols = [(j, min(128, Dc - j), j) for j in range(0, Dc, 128)]
                s.linear(ps, lambda k0, k1, c0, c1, wp=wp: wp.ap()[l, k0:k1, c0:c1], Kb, cols,
                         lambda tt, yt=yt, yn=yn, Kb=Kb: (yt.ap()[0:Kb, tt * NT:(tt + 1) * NT], dd[yn]), T, epi, wblk=Dc)
            s.barrier()
        s.allgather(s.zb, dd["zb"], s.zT, dd["zT"])
        with ExitStack() as ps:
            xa = Ring(P, ps, 2, [128, NT], F32, "gxa"); xo = Ring(P, ps, 2, [128, NT], F32, "gxo")

            def epi2(tag, bk, bd, m, tt):
                j = tag
                tsl = slice(tt * NT, (tt + 1) * NT)
                a_, ad = xa.next(); o_, od = xo.next()
                P.dma("sp", a_[0:m, :], xcur.ap()[j:j + m, tsl], [xcur_d], [ad], ad)
                P.op("dve", lambda e: e.tensor_tensor(out=o_[0:m, :], in0=bk[0:m, 0:NT], in1=a_[0:m, :], op=ALU.add), [bd, ad], [od])
                P.dma("sp", xnext.ap()[j:j + m, tsl], o_[0:m, :], [od], [xnext_d], od)
            cols = [(j, min(128, Dc - j), j) for j in range(0, Dc, 128)]
            s.linear(ps, lambda k0, k1, c0, c1: s.wout.ap()[l, k0:k1, c0:c1], D, cols,
                     lambda tt: (s.zT.ap()[:, tt * NT:(tt + 1) * NT], dd["zT"]), T, epi2, wblk=Dc)

    def phase_norm(s, xcur, xcur_d, gain, final):
        c = s.c; P = s.P; NT = c.NT; T = c.SEQ; NTT = T // NT; dd = s.dd; D = c.D; DcC = D // NCORES // 128
        ssp, ssg, hb, hT = s.ssp, s.ssg, s.hb, s.hT
        with ExitStack() as ps:
            xs = Ring(P, ps, 2, [128, DcC, NT], F32, "xs")
            sq = Ring(P, ps, 2, [128, DcC, NT], F32, "sq")
            row = Ring(P, ps, 2, [1, NT], F32, "row")
            for tt in range(NTT):
                xt, xd = xs.next(); qt, qd = sq.next(); rt, rd = row.next()
                tsl = slice(tt * NT, (tt + 1) * NT)
                P.dma("sp", xt[:], xcur.ap()[:, tsl].rearrange("(kc p) n -> p kc n", p=128), [xcur_d], [xd], xd)
                P.op("act", (lambda xt=xt, qt=qt: lambda e: e.activation(out=qt[:], in_=xt[:], func=AF.Square))(), [xd], [qd])
                bk, bd = s.bank()
                for kc in range(DcC):
                    P.op("pe", (lambda bk=bk, qt=qt, kc=kc: lambda e: e.matmul(
                        bk[:, 0:NT], lhsT=s.ones_f[:], rhs=qt[:, kc, :], start=(kc == 0), stop=(kc == DcC - 1)))(),
                        [s.ones_d, qd], [bd], inc=(kc == DcC - 1))
                P.op("dve", (lambda bk=bk, rt=rt: lambda e: e.tensor_copy(out=rt[:], in_=bk[0:1, 0:NT]))(), [bd], [rd])
                P.dma("sp", ssp.ap()[:, tsl], rt[:], [rd], [dd["ssp"]], rd)
            s.allgather(ssp, dd["ssp"], ssg, dd["ssg"])
            g8 = Ring(P, ps, 2, [NCORES, NT], F32, "g8")
            rs = Ring(P, ps, 2, [128, NT], F32, "rs")
            hs = Ring(P, ps, 2, [128, DcC, NT], F32 if final else BF16, "hs")
            for tt in range(NTT):
                xt, xd = xs.next(); gt, gd = g8.next(); rt, rd = rs.next(); ht, hd = hs.next()
                tsl = slice(tt * NT, (tt + 1) * NT)
                P.dma("sp", gt[:], ssg.ap()[:, tsl], [dd["ssg"]], [gd], gd)
                P.dma("sp", xt[:], xcur.ap()[:, tsl].rearrange("(kc p) n -> p kc n", p=128), [xcur_d], [xd], xd)
                bk, bd = s.bank()
                P.op("pe", (lambda bk=bk, gt=gt: lambda e: e.matmul(bk[:, 0:NT], lhsT=s.ones_f[0:NCORES, :], rhs=gt[:],
                                                                    start=True, stop=True))(), [s.ones_d, gd], [bd])
                P.op("act", (lambda bk=bk, rt=rt: lambda e: e.activation(out=rt[:], in_=bk[:, 0:NT], func=AF.Sqrt,
                                                                         bias=1e-6, scale=1.0 / D))(), [bd], [rd])
                P.op("dve", (lambda rt=rt: lambda e: e.reciprocal(out=rt[:], in_=rt[:]))(), [rd], [rd])
                for kc in range(DcC):
                    P.op("dve", (lambda xt=xt, ht=ht, rt=rt, kc=kc: lambda e: e.scalar_tensor_tensor(
                        out=ht[:, kc, :], in0=xt[:, kc, :], scalar=gain(kc), in1=rt[:],
                        op0=ALU.mult, op1=ALU.mult))(), [xd, rd, s.const_d], [hd])
                if final:
                    P.dma("sp", s.yT.ap()[:, tsl].rearrange("(kc p) n -> p kc n", p=128), ht[:], [hd], [dd["yT"]], hd)
                else:
                    P.dma("sp", hb.ap()[:, tsl].rearrange("(kc p) n -> p kc n", p=128), ht[:], [hd], [dd["hb"]], hd)
            if not final:
                s.allgather(hb, dd["hb"], hT, dd["hT"])

    def phase_mlstm(s, l):
        c = s.c; P = s.P; NT = c.NT; T = c.SEQ; dd = s.dd; NCH = T // 128; CPB = NT // 128
        AX = mybir.AxisListType.X
        nst = 0
        while (1 << nst) < NCH:
            nst += 1
        with ExitStack() as ps:
            E1T = [P.sb(ps, [128, NCH], F32, "E1T%d" % d) for d in range(2)]
            THT = [P.sb(ps, [128, NCH], F32, "THT%d" % d) for d in range(2)]
            DEC = [P.sb(ps, [128, NCH], F32, "DEC%d" % d) for d in range(2)]
            gd_ = [Dep("gates%d" % d) for d in range(2)]
            with ExitStack() as gs:
                tl = lambda n, shp=None: (P.sb(gs, shp or [NCH, 128], F32, n), Dep(n))
                def _gates(d):
                    rev = (d == 1)
                    (mi, mid), (mf, mfd), (A_, Ad), (B_, Bd), (u_, ud), (g1, g1d) = tl("mi"), tl("mf"), tl("A"), tl("B"), tl("u"), tl("g1")
                    (col, cold) = tl("col", [NCH, 4]); (r0, r0d), (r1, r1d) = tl("r0", [1, NCH]), tl("r1", [1, NCH])
                    (rr, rrd) = tl("rr", [1, 2, NCH]); (dg, dgd) = tl("dg", [NCH, NCH])
                    P.dma("sp", mi[:], s.inter["mif"].ap()[2 * d:2 * d + 1, :].rearrange("o (c j) -> (o c) j", j=128), [dd["p1_mif"]], [mid], mid)
                    P.dma("sp", mf[:], s.inter["mif"].ap()[2 * d + 1:2 * d + 2, :].rearrange("o (c j) -> (o c) j", j=128), [dd["p1_mif"]], [mfd], mfd)
                    P.op("dve", lambda e: e.tensor_scalar(out=mi[:], in0=mi[:], scalar1=s.gb_sb[0:NCH, l, 2 * d:2 * d + 1], scalar2=None, op0=ALU.add), [mid, s.const_d], [mid])
                    P.op("dve", lambda e: e.tensor_scalar(out=mf[:], in0=mf[:], scalar1=s.gb_sb[0:NCH, l, 2 * d + 1:2 * d + 2], scalar2=None, op0=ALU.add), [mfd, s.const_d], [mfd])
                    P.op("act", lambda e: e.activation(out=mf[:], in_=mf[:], func=AF.Exp, scale=-1.0), [mfd], [mfd])
                    P.op("act", lambda e: e.activation(out=A_[:], in_=mf[:], func=AF.Ln, bias=1.0, scale=1.0), [mfd], [Ad])
                    src, srcd, dst, dstd = A_, Ad, B_, Bd
                    for k in range(7):
                        sh = 1 << k
                        if not rev:
                            P.op("dve", (lambda src=src, dst=dst, sh=sh: lambda e: e.tensor_tensor(out=dst[:, sh:], in0=src[:, sh:], in1=src[:, :128 - sh], op=ALU.add))(), [srcd], [dstd])
                            P.op("dve", (lambda src=src, dst=dst, sh=sh: lambda e: e.tensor_copy(out=dst[:, :sh], in_=src[:, :sh]))(), [srcd], [dstd])
                        else:
                            P.op("dve", (lambda src=src, dst=dst, sh=sh: lambda e: e.tensor_tensor(out=dst[:, :128 - sh], in0=src[:, :128 - sh], in1=src[:, sh:], op=ALU.add))(), [srcd], [dstd])
                            P.op("dve", (lambda src=src, dst=dst, sh=sh: lambda e: e.tensor_copy(out=dst[:, 128 - sh:], in_=src[:, 128 - sh:]))(), [srcd], [dstd])
                        src, srcd, dst, dstd = dst, dstd, src, srcd
                    cs_, csd = src, srcd
                    tcol = (lambda cs_=cs_: cs_[:, 127:128] if not rev else cs_[:, 0:1])()
                    bk, bd = s.bank()
                    tri = s.Lm if not rev else s.Um
                    P.op("act", lambda e: e.activation(out=col[:, 0:1], in_=tcol, func=AF.Copy), [csd], [cold])
                    P.op("pe", lambda e: e.matmul(bk[0:NCH, 0:1], lhsT=tri[0:NCH, 0:NCH], rhs=col[:, 0:1], start=True, stop=True), [s.const_d, cold], [bd])
                    P.op("act", lambda e: e.activation(out=col[:, 1:2], in_=bk[0:NCH, 0:1], func=AF.Copy), [bd], [cold])
                    P.op("dve", lambda e: e.tensor_scalar(out=g1[:], in0=cs_[:], scalar1=col[:, 1:2], scalar2=None, op0=ALU.add), [csd, cold], [g1d])
                    P.op("dve", lambda e: e.tensor_tensor(out=u_[:], in0=mi[:], in1=g1[:], op=ALU.add), [mid, g1d], [ud])
                    P.op("dve", lambda e: e.tensor_reduce(out=col[:, 2:3], in_=u_[:], axis=AX, op=ALU.max), [ud], [cold])
                    bk2, bd2 = s.bank()
                    P.op("pe", lambda e: e.matmul(bk2[0:1, 0:NCH], lhsT=col[:, 2:3], rhs=s.id_f[0:NCH, 0:NCH], start=True, stop=True), [s.const_d, cold], [bd2])
                    P.op("act", lambda e: e.activation(out=r0[:], in_=bk2[0:1, 0:NCH], func=AF.Copy), [bd2], [r0d])
                    src, srcd, dst, dstd = r0, r0d, r1, r1d
                    for k in range(nst):
                        sh = 1 << k
                        if not rev:
                            P.op("dve", (lambda src=src, dst=dst, sh=sh: lambda e: e.tensor_tensor(out=dst[:, sh:], in0=src[:, sh:], in1=src[:, :NCH - sh], op=ALU.max))(), [srcd], [dstd])
                            P.op("dve", (lambda src=src, dst=dst, sh=sh: lambda e: e.tensor_copy(out=dst[:, :sh], in_=src[:, :sh]))(), [srcd], [dstd])
                        else:
                            P.op("dve", (lambda src=src, dst=dst, sh=sh: lambda e: e.tensor_tensor(out=dst[:, :NCH - sh], in0=src[:, :NCH - sh], in1=src[:, sh:], op=ALU.max))(), [srcd], [dstd])
                            P.op("dve", (lambda src=src, dst=dst, sh=sh: lambda e: e.tensor_copy(out=dst[:, NCH - sh:], in_=src[:, NCH - sh:]))(), [srcd], [dstd])
                        src, srcd, dst, dstd = dst, dstd, src, srcd
                    pin, pind = src, srcd
                    P.op("dve", lambda e: e.memset(rr[:], 0.0), [], [rrd])
                    P.op("dve", lambda e: e.tensor_scalar(out=rr[:, 0, :], in0=pin[:], scalar1=0.0, scalar2=None, op0=ALU.max), [pind], [rrd])
                    if not rev:
                        P.op("dve", lambda e: e.tensor_copy(out=rr[:, 1, 1:NCH], in_=rr[:, 0, 0:NCH - 1]), [rrd], [rrd])
                    else:
                        P.op("dve", lambda e: e.tensor_copy(out=rr[:, 1, 0:NCH - 1], in_=rr[:, 0, 1:NCH]), [rrd], [rrd])
                    bk3, bd3 = s.bank()
                    P.op("pe", lambda e: e.matmul(bk3[0:NCH, 0:1], lhsT=rr[:, 0, :], rhs=s.ones_f[0:1, 0:1], start=True, stop=True), [s.ones_d, rrd], [bd3])
                    P.op("pe", lambda e: e.matmul(bk3[0:NCH, 1:2], lhsT=rr[:, 1, :], rhs=s.ones_f[0:1, 0:1], start=True, stop=True), [s.ones_d, rrd], [bd3])
                    P.op("dve", lambda e: e.tensor_scalar(out=col[:, 2:3], in0=bk3[0:NCH, 0:1], scalar1=-1.0, scalar2=None, op0=ALU.mult), [bd3], [cold])
                    P.op("dve", lambda e: e.tensor_tensor(out=col[:, 3:4], in0=bk3[0:NCH, 1:2], in1=col[:, 2:3], op=ALU.add), [bd3, cold], [cold])
                    P.op("act", lambda e: e.activation(out=u_[:], in_=u_[:], func=AF.Exp, bias=col[:, 2:3], scale=1.0), [ud, cold], [ud])
                    P.op("act", lambda e: e.activation(out=g1[:], in_=g1[:], func=AF.Exp, bias=col[:, 2:3], scale=1.0), [g1d, cold], [g1d])
                    P.op("act", lambda e: e.activation(out=col[:, 3:4], in_=col[:, 3:4], func=AF.Exp), [cold], [cold])
                    P.op("dve", lambda e: e.tensor_scalar(out=dg[:], in0=s.id_f[0:NCH, 0:NCH], scalar1=col[:, 3:4], scalar2=None, op0=ALU.mult), [s.const_d, cold], [dgd])
                    b4, b4d = s.bank(); b5, b5d = s.bank(); b6, b6d = s.bank()
                    P.op("pe", lambda e: e.matmul(b4[:, 0:NCH], lhsT=u_[:], rhs=s.id_f[0:NCH, 0:NCH], start=True, stop=True), [ud, s.const_d], [b4d])
                    P.op("pe", lambda e: e.matmul(b5[:, 0:NCH], lhsT=g1[:], rhs=s.id_f[0:NCH, 0:NCH], start=True, stop=True), [g1d, s.const_d], [b5d])
                    P.op("pe", lambda e: e.matmul(b6[:, 0:NCH], lhsT=s.ones_f[0:NCH, :], rhs=dg[:], start=True, stop=True), [dgd, s.ones_d], [b6d])
                    P.op("act", lambda e: e.activation(out=E1T[d][:], in_=b4[:, 0:NCH], func=AF.Copy), [b4d], [gd_[d]])
                    P.op("act", lambda e: e.activation(out=THT[d][:], in_=b5[:, 0:NCH], func=AF.Copy), [b5d], [gd_[d]])
                    P.op("act", lambda e: e.activation(out=DEC[d][:], in_=b6[:, 0:NCH], func=AF.Copy), [b6d], [gd_[d]])
                for d in range(2):
                    _gates(d)
                s.barrier()
            Cst = P.sb(ps, [128, 2, 512], F32, "Cst"); Cb = P.sb(ps, [128, 2, 512], BF16, "Cb"); Cd = Dep("C")
            nst_ = P.sb(ps, [128, 2], F32, "nst"); nb = P.sb(ps, [128, 2], BF16, "nb"); nd = Dep("n")
            qb = Ring(P, ps, 2, [128, 2, NT], BF16, "lq"); kb = Ring(P, ps, 2, [128, 2, NT], BF16, "lk"); vb = Ring(P, ps, 2, [128, 4, NT], BF16, "lv")
            gob = Ring(P, ps, 2, [128, 4, NT], BF16, "lgo"); ggb = Ring(P, ps, 2, [128, 4, NT], BF16, "lgg")
            wk_ = Ring(P, ps, 2, [128, 256], BF16, "lwk"); vt_ = Ring(P, ps, 2, [128, 512], BF16, "lvt"); sp_ = Ring(P, ps, 2, [128, 128], BF16, "lsp")
            ar = Ring(P, ps, 2, [128, 2], F32, "lar"); hf = Ring(P, ps, 2, [128, 512], F32, "lhf"); hs = Ring(P, ps, 2, [128, 512], F32, "lhs")
            jk = Ring(P, ps, 2, [128, 512], F32, "ljk"); yb_ = Ring(P, ps, 2, [128, 512], BF16, "lyb"); yo = Ring(P, ps, 2, [128, 4, 128], F32, "lyo")
            yo2 = Ring(P, ps, 2, [128, 4, 128], BF16, "lyo2")
            for d in range(2):
                rev = (d == 1)
                mask = s.mk_f if not rev else s.mk_b
                P.op("pool", lambda e: e.memset(Cst[:], 0.0), [], [Cd]); P.op("pool", lambda e: e.memset(Cb[:], 0.0), [], [Cd])
                P.op("pool", lambda e: e.memset(nst_[:], 0.0), [], [nd]); P.op("pool", lambda e: e.memset(nb[:], 0.0), [], [nd])
                order = list(range(NCH)) if not rev else list(range(NCH - 1, -1, -1))
                cur_blk = None
                for oi, ch in enumerate(order):
                    blk = ch // CPB
                    if blk != cur_blk:
                        cur_blk = blk
                        tsl = slice(blk * NT, (blk + 1) * NT)
                        q_, qd = qb.next(); k_, kd = kb.next(); v_, vd = vb.next()
                        P.dma("sp", q_[:], s.inter["mq"].ap()[:, tsl].rearrange("(t p) n -> p t n", p=128), [dd["p1_mq"]], [qd], qd)
                        P.dma("sp", k_[:], s.inter["mk"].ap()[:, tsl].rearrange("(t p) n -> p t n", p=128), [dd["p1_mk"]], [kd], kd)
                        P.dma("sp", v_[:], s.inter["mv"].ap()[:, tsl].rearrange("(t p) n -> p t n", p=128), [dd["p1_mv"]], [vd], vd)
                        if rev:
                            go_, god = gob.next(); gg_, ggd = ggb.next()
                            P.dma("sp", go_[:], s.inter["mo"].ap()[:, tsl].rearrange("(t p) n -> p t n", p=128), [dd["p1_mo"]], [god], god)
                            P.dma("sp", gg_[:], s.inter["mgate"].ap()[:, tsl].rearrange("(t p) n -> p t n", p=128), [dd["p1_mgate"]], [ggd], ggd)
                    cs = slice((ch % CPB) * 128, (ch % CPB + 1) * 128)
                    e1c = E1T[d][:, ch:ch + 1]
                    kt, ktd = s.bank(); vtk, vtd = s.bank()
                    for t in range(2):
                        P.op("pe", (lambda t=t, kt=kt, k_=k_, cs=cs: lambda e: e.matmul(kt[:, t * 128:(t + 1) * 128], lhsT=k_[:, t, cs], rhs=s.id_b[:], start=True, stop=True))(), [kd, s.const_d], [ktd], inc=(t == 1))
                    for t in range(4):
                        P.op("pe", (lambda t=t, vtk=vtk, v_=v_, cs=cs: lambda e: e.matmul(vtk[:, t * 128:(t + 1) * 128], lhsT=v_[:, t, cs], rhs=s.id_b[:], start=True, stop=True))(), [vd, s.const_d], [vtd], inc=(t == 3))
                    w_, wd = wk_.next(); vt, vtd2 = vt_.next(); sp, spd = sp_.next()
                    P.op("dve", (lambda w_=w_, kt=kt, e1c=e1c: lambda e: e.tensor_scalar(out=w_[:], in0=kt[:, 0:256], scalar1=e1c, scalar2=None, op0=ALU.mult))(), [ktd, gd_[d]], [wd])
                    P.op("act", (lambda vt=vt, vtk=vtk: lambda e: e.activation(out=vt[:], in_=vtk[:, 0:512], func=AF.Copy))(), [vtd], [vtd2])
                    sc, scd = s.bank()
                    for t in range(2):
                        P.op("pe", (lambda t=t, sc=sc, k_=k_, q_=q_, cs=cs: lambda e: e.matmul(sc[:, 0:128], lhsT=k_[:, t, cs], rhs=q_[:, t, cs], start=(t == 0), stop=(t == 1)))(), [kd, qd], [scd], inc=(t == 1))
                    P.op("dve", (lambda sp=sp, sc=sc, e1c=e1c, mask=mask: lambda e: e.scalar_tensor_tensor(out=sp[:], in0=sc[:, 0:128], scalar=e1c, in1=mask[:], op0=ALU.mult, op1=ALU.mult))(), [scd, gd_[d], s.const_d], [spd])
                    nm, nmd = s.bank(); dn, dnd = s.bank()
                    P.op("pe", (lambda nm=nm, sp=sp, vt=vt: lambda e: e.matmul(nm[:, 0:512], lhsT=sp[:], rhs=vt[:], start=True, stop=False))(), [spd, vtd2], [nmd], inc=False)
                    for t in range(2):
                        P.op("pe", (lambda t=t, nm=nm, q_=q_, cs=cs: lambda e: e.matmul(nm[:, 0:512], lhsT=q_[:, t, cs], rhs=Cb[:, t, :], start=False, stop=(t == 1)))(), [qd, Cd], [nmd], inc=(t == 1))
                    P.op("pe", (lambda dn=dn, sp=sp: lambda e: e.matmul(dn[:, 0:1], lhsT=sp[:], rhs=s.ones_b[:, 0:1], start=True, stop=False))(), [spd, s.const_d], [dnd], inc=False)
                    for t in range(2):
                        P.op("pe", (lambda t=t, dn=dn, q_=q_, cs=cs: lambda e: e.matmul(dn[:, 0:1], lhsT=q_[:, t, cs], rhs=nb[:, t:t + 1], start=False, stop=(t == 1)))(), [qd, nd], [dnd], inc=(t == 1))
                    a_, ad = ar.next()
                    P.op("act", (lambda a_=a_, dn=dn: lambda e: e.activation(out=a_[:, 0:1], in_=dn[:, 0:1], func=AF.Abs))(), [dnd], [ad])
                    P.op("dve", (lambda a_=a_, ch=ch, d=d: lambda e: e.tensor_tensor(out=a_[:, 0:1], in0=a_[:, 0:1], in1=THT[d][:, ch:ch + 1], op=ALU.max))(), [ad, gd_[d]], [ad])
                    P.op("dve", (lambda a_=a_: lambda e: e.reciprocal(out=a_[:, 0:1], in_=a_[:, 0:1]))(), [ad], [ad])
                    tok = slice(ch * 128, (ch + 1) * 128)
                    if not rev:
                        h_, hd_ = hf.next()
                        P.op("dve", (lambda h_=h_, nm=nm, a_=a_: lambda e: e.tensor_scalar(out=h_[:], in0=nm[:, 0:512], scalar1=a_[:, 0:1], scalar2=None, op0=ALU.mult))(), [nmd, ad], [hd_])
                        P.dma("sp", s.hdir.ap()[tok, :], h_[:], [hd_], [dd["hdir"]], hd_)
                    else:
                        h_, hd_ = hf.next(); x_, xd_ = hs.next(); j_, jd = jk.next(); y_, yd_ = yb_.next(); o_, od_ = yo.next(); o2, o2d = yo2.next()
                        P.dma("sp", h_[:], s.hdir.ap()[tok, :], [dd["hdir"]], [hd_], hd_)
                        P.op("dve", (lambda x_=x_, nm=nm, a_=a_, h_=h_: lambda e: e.scalar_tensor_tensor(out=x_[:], in0=nm[:, 0:512], scalar=a_[:, 0:1], in1=h_[:], op0=ALU.mult, op1=ALU.add))(), [nmd, ad, hd_], [xd_])
                        P.op("act", (lambda j_=j_, x_=x_, a_=a_: lambda e: e.activation(out=j_[:], in_=x_[:], func=AF.Square, accum_out=a_[:, 1:2]))(), [xd_, ad], [jd, ad])
                        P.op("act", (lambda a_=a_: lambda e: e.activation(out=a_[:, 1:2], in_=a_[:, 1:2], func=AF.Sqrt, bias=1e-6, scale=1.0 / 512))(), [ad], [ad])
                        P.op("dve", (lambda a_=a_: lambda e: e.reciprocal(out=a_[:, 1:2], in_=a_[:, 1:2]))(), [ad], [ad])
                        P.op("dve", (lambda y_=y_, x_=x_, a_=a_: lambda e: e.scalar_tensor_tensor(out=y_[:], in0=x_[:], scalar=a_[:, 1:2], in1=s.mng_sb[:, l, :], op0=ALU.mult, op1=ALU.mult))(), [xd_, ad, s.const_d], [yd_])
                        yt, ytd = s.bank()
                        for t in range(4):
                            P.op("pe", (lambda t=t, yt=yt, y_=y_: lambda e: e.matmul(yt[:, t * 128:(t + 1) * 128], lhsT=y_[:, t * 128:(t + 1) * 128], rhs=s.id_b[:], start=True, stop=True))(), [yd_, s.const_d], [ytd], inc=(t == 3))
                        P.op("dve", (lambda o_=o_, yt=yt, go_=go_, cs=cs: lambda e: e.tensor_tensor(out=o_[:], in0=yt[:, 0:512].rearrange("p (t n) -> p t n", t=4), in1=go_[:, :, cs], op=ALU.mult))(), [ytd, god], [od_])
                        P.op("dve", (lambda o_=o_, o2=o2, gg_=gg_, cs=cs: lambda e: e.tensor_tensor(out=o2[:], in0=o_[:], in1=gg_[:, :, cs], op=ALU.mult))(), [od_, ggd], [o2d])
                        P.dma("sp", s.yc_b.ap()[:, tok].rearrange("(t p) n -> p t n", p=128), o2[:], [o2d], [dd["yc_b"]], o2d)
                    if oi < NCH - 1:
                        nxt = order[oi + 1]
                        dcn = DEC[d][:, nxt:nxt + 1]
                        for t in range(2):
                            ub, ubd = s.bank()
                            P.op("pe", (lambda t=t, ub=ub, w_=w_, vt=vt: lambda e: e.matmul(ub[:, 0:512], lhsT=w_[:, t * 128:(t + 1) * 128], rhs=vt[:], start=True, stop=True))(), [wd, vtd2], [ubd])
                            P.op("dve", (lambda t=t, ub=ub: lambda e: e.tensor_tensor(out=Cst[:, t, :], in0=ub[:, 0:512], in1=Cst[:, t, :], op=ALU.add))(), [ubd, Cd], [Cd])
                            P.op("dve", (lambda t=t, dcn=dcn: lambda e: e.tensor_scalar(out=Cst[:, t, :], in0=Cst[:, t, :], scalar1=dcn, scalar2=None, op0=ALU.mult))(), [Cd, gd_[d]], [Cd])
                            P.op("act", (lambda t=t: lambda e: e.activation(out=Cb[:, t, :], in_=Cst[:, t, :], func=AF.Copy))(), [Cd], [Cd])
                        nu, nud = s.bank()
                        for t in range(2):
                            P.op("pe", (lambda t=t, nu=nu, w_=w_: lambda e: e.matmul(nu[:, t:t + 1], lhsT=w_[:, t * 128:(t + 1) * 128], rhs=s.ones_b[:, 0:1], start=True, stop=True))(), [wd, s.const_d], [nud], inc=(t == 1))
                        P.op("dve", (lambda nu=nu: lambda e: e.tensor_tensor(out=nst_[:], in0=nu[:, 0:2], in1=nst_[:], op=ALU.add))(), [nud, nd], [nd])
                        P.op("dve", (lambda dcn=dcn: lambda e: e.tensor_scalar(out=nst_[:], in0=nst_[:], scalar1=dcn, scalar2=None, op0=ALU.mult))(), [nd, gd_[d]], [nd])
                        P.op("act", lambda e: e.activation(out=nb[:], in_=nst_[:], func=AF.Copy), [nd], [nd])

    def build(s):
        c = s.c; nc = s.nc; P = s.P
        L = c.DEPTH; T = c.SEQ; NT = c.NT; D = c.D; KC = c.KC; Dc = D // NCORES; DcC = Dc // 128
        NTT = T // NT
        st = s.stack
        seg = [("cq", c.QR // NCORES), ("ckv", c.KVR // NCORES), ("kr", c.ROPE // NCORES), ("agate", 256), ("glua", 256), ("glug", 256),
               ("bgate", 256), ("mq", 256), ("mk", 256), ("mv", 512), ("mo", 512), ("mgate", 512), ("mif", 4),
               ("dq", 128), ("mg0", Dc), ("mg1", Dc), ("mg2", Dc), ("mg3", Dc)]
        soff = {}; o = 0
        for n, z in seg:
            soff[n] = (o, z); o += z
        NC1 = o
        s.NC1 = NC1; s.soff = soff
        xT = s.din("xT", [Dc, T])
        normg = s.din("normg", [128, L, DcC])
        w1 = s.din("w1", [L, D, NC1])
        yT = s.dout("yT", [Dc, T])
        ones_f = P.sb(st, [128, 128], F32, "ones"); ones_d = Dep("ones")
        P.op("pool", lambda e: e.memset(ones_f[:], 1.0), [], [ones_d])
        ng = P.sb(st, [128, L, DcC], F32, "ng"); ng_d = Dep("ng")
        P.dma("sp", ng[:], normg.ap(), [], [ng_d], ng_d)
        s.ng_d = ng_d

        ssp = s.dint("ssp", [1, T], F32); ssg = s.dint("ssg", [NCORES, T], F32, shared=True)
        hb = s.dint("hb", [Dc, T], BF16); hT = s.dint("hT", [D, T], BF16, shared=True)
        xcur = xT; xcur_d = s.dd["xT"]
        inter = {}
        for n, z in seg:
            dt = F32 if n in ("cq", "ckv", "kr", "mif") else BF16
            inter[n] = s.dint("p1_" + n, [z, T], dt)

        s.inter = inter; s.ones_f = ones_f; s.ones_d = ones_d
        s.pos = s.din("pos", [1, T], I32)
        cqg = s.din("cqg", [128, L, c.QR // 128]); ckvg = s.din("ckvg", [128, L, c.KVR // 128])
        s.wuq = s.din("wuq", [L, c.QR, 384]); s.wukv = s.din("wukv", [L, c.KVR, 512])
        invf = s.din("invf", [64, 1]); rotm = s.din("rotm", [64, 64])
        convw = s.din("convw", [128, L, 2, c.CONVW]); convb = s.din("convb", [128, L, 2])
        lng = s.din("lng", [128, L, 2]); lnb = s.din("lnb", [128, L, 2])
        sel12 = s.din("sel12", [2 * NCORES, 2, 128])
        s.const_d = Dep("const")
        def cload(name, t, shape, dt=F32):
            sb_ = P.sb(st, shape, dt, name)
            P.dma("sp", sb_[:], t.ap(), [], [s.const_d], s.const_d)
            return sb_
        s.cqg_sb = cload("cqg", cqg, [128, L, c.QR // 128]); s.ckvg_sb = cload("ckvg", ckvg, [128, L, c.KVR // 128])
        s.invf_sb = cload("invf", invf, [64, 1]); s.rot_sb = cload("rot", rotm, [64, 64])
        s.cw_sb = cload("cw", convw, [128, L, 2, c.CONVW]); s.cb_sb = cload("cb", convb, [128, L, 2])
        s.lg_sb = cload("lg", lng, [128, L, 2]); s.lb_sb = cload("lb", lnb, [128, L, 2])
        sel_sb = cload("sel", sel12, [2 * NCORES, 2, 128])
        s.sel1 = sel_sb[:, 0, :]; s.sel2 = sel_sb[:, 1, :]
        s.halfpi = P.sb(st, [64, 1], F32, "halfpi")
        P.op("pool", lambda e: e.memset(s.halfpi[:], math.pi / 2), [], [s.const_d])
        s.ones_b = P.sb(st, [128, 128], BF16, "onesb")
        P.op("pool", lambda e: e.memset(s.ones_b[:], 1.0), [], [s.const_d])
        s.cqn = s.dint("cqn", [c.QR, T], BF16); s.ckvn = s.dint("ckvn", [c.KVR, T], BF16)
        s.qn = s.dint("qn", [256, T], BF16); s.qr = s.dint("qr", [128, T], BF16)
        s.kn = s.dint("kn", [256, T], BF16); s.krp = s.dint("krp", [64, T], BF16)
        s.vtok = s.dint("vtok", [T, 256], BF16)
        s.ya_b = s.dint("ya_b", [256, T], BF16); s.yb_b = s.dint("yb_b", [256, T], BF16)
        s.cvo = s.dint("cvo", [256, T], F32)
        s.lnp = s.dint("lnp", [2, T], F32); s.lng_ = s.dint("lng_", [2 * NCORES, T], F32, shared=True)
        s.ssp, s.ssg, s.hb, s.hT, s.yT = ssp, ssg, hb, hT, yT
        s.const_d2 = ng_d
        s.memT = s.din("memT", [D, c.N_MEM]); memg = s.din("memg", [128, L, KC]); s.wmkv = s.din("wmkv", [L, D, 256])
        s.wpa = s.din("wpa", [L, c.BW, Dc]); s.wpb = s.din("wpb", [L, c.BW, Dc]); s.wpc = s.din("wpc", [L, c.BW, Dc])
        s.wpd = s.din("wpd", [L, c.DW, Dc]); s.wout = s.din("wout", [L, D, Dc])
        fng = s.din("fng", [128, DcC]); gb = s.din("gb", [128, L, 4]); mng = s.din("mng", [128, L, 512])
        cm = s.din("cmats", [128, 5, 128])
        s.memg_sb = cload("memg", memg, [128, L, KC]); s.fng_sb = cload("fng", fng, [128, DcC])
        s.gb_sb = cload("gb", gb, [128, L, 4]); s.mng_sb = cload("mng", mng, [128, L, 512])
        cm_sb = cload("cm", cm, [128, 5, 128])
        s.id_f = cm_sb[:, 0, :]; s.mk_f = cm_sb[:, 1, :]; s.mk_b = cm_sb[:, 2, :]; s.Lm = cm_sb[:, 3, :]; s.Um = cm_sb[:, 4, :]
        s.id_b = P.sb(st, [128, 128], BF16, "idb")
        P.op("pool", lambda e: e.tensor_copy(out=s.id_b[:], in_=cm_sb[:, 0, :]), [s.const_d], [s.const_d])
        s.memn = s.dint("memn", [D, c.N_MEM], BF16)
        s.yc_b = s.dint("yc_b", [512, T], BF16); s.yd_b = s.dint("yd_b", [128, T], BF16)
        s.hdir = s.dint("hdir", [T, 512], F32)
        s.ybuf = s.dint("ybuf", [NCORES * 512, T], BF16, shared=True)
        s.cq_g = s.dint("cq_g", [c.QR, T], F32, shared=True); s.ckv_g = s.dint("ckv_g", [c.KVR, T], F32, shared=True)
        s.kr_g = s.dint("kr_g", [c.ROPE, T], F32, shared=True)
        s.zacc = s.dint("zacc", [Dc, T], F32); s.zb = s.dint("zb", [Dc, T], BF16)
        s.zT = hT; s.dd["zT"] = s.dd["hT"]
        s.xa = s.dint("xa", [Dc, T], F32); s.xb = s.dint("xb", [Dc, T], F32)
        s.rope_tables()
        s.barrier()

        for l in range(L):
            s.phase_norm(xcur, xcur_d, (lambda kc, l=l: ng[:, l, kc:kc + 1]), False)
            s.barrier()
            with ExitStack() as ps:
                og = Ring(P, ps, 4, [128, NT], F32, "og")
                ogb = Ring(P, ps, 4, [128, NT], BF16, "ogb")
                cols = []
                for n, z in seg:
                    o0 = soff[n][0]
                    for j in range(0, z, 128):
                        cols.append((o0 + j, min(128, z - j), (n, j)))

                def epi(tag, bk, bd, m, tt, l=l):
                    n, j = tag
                    tsl = slice(tt * NT, (tt + 1) * NT)
                    dst = inter[n]
                    if dst.dtype == F32:
                        t_, d_ = og.next()
                    else:
                        t_, d_ = ogb.next()
                    if n in ("agate", "bgate", "mgate"):
                        f = AF.Silu
                    elif n in ("glug", "mo", "mg0", "mg1", "mg2", "mg3"):
                        f = AF.Sigmoid
                    else:
                        f = AF.Copy
                    sc = (c.DK ** -0.5) if n == "mk" else 1.0
                    P.op("act", (lambda t_=t_, bk=bk, m=m, f=f, sc=sc: lambda e: e.activation(
                        out=t_[0:m, :], in_=bk[0:m, 0:NT], func=f, scale=sc))(), [bd], [d_])
                    P.dma("act", dst.ap()[j:j + m, tsl], t_[0:m, :], [d_], [s.dd["p1_" + n]], d_)

                s.linear(ps, lambda k0, k1, c0, c1, l=l: w1.ap()[l, k0:k1, c0:c1], D, cols,
                         lambda tt: (hT.ap()[:, tt * NT:(tt + 1) * NT], s.dd["hT"]), T, epi)
            s.barrier()

            s.allgather(inter["cq"], s.dd["p1_cq"], s.cq_g, s.dd["cq_g"])
            s.allgather(inter["ckv"], s.dd["p1_ckv"], s.ckv_g, s.dd["ckv_g"])
            s.allgather(inter["kr"], s.dd["p1_kr"], s.kr_g, s.dd["kr_g"])
            s.phase_attn(l)
            s.barrier()
            s.phase_conv(l)
            s.barrier()
            if "p2" in s.debug:
                break
            s.phase_mlstm(l)
            s.barrier()
            s.phase_mem(l)
            s.barrier()
            if "p3" in s.debug:
                break
            xnext, xnext_d = (s.xa, s.dd["xa"]) if l % 2 == 0 else (s.xb, s.dd["xb"])
            s.phase_out(l, xcur, xcur_d, xnext, xnext_d)
            s.barrier()
            xcur, xcur_d = xnext, xnext_d
        if not (s.debug & {"p1", "p2", "p3"}):
            s.phase_norm(xcur, xcur_d, (lambda kc: s.fng_sb[:, kc:kc + 1]), True)
        if "p1" in s.debug:
            dh = s.dout("dbg_hT", [D, T], BF16)
            P.dma("sp", dh.ap(), hT.ap(), [s.dd["hT"]], [s.dd["dbg_hT"]], s.dd["dbg_hT"])
            for n in ("agate", "mk", "mif", "mg3"):
                z = soff[n][1]
                dn = s.dout("dbg_" + n, [z, T], inter[n].dtype)
                P.dma("sp", dn.ap(), inter[n].ap(), [s.dd["p1_" + n]], [s.dd["dbg_" + n]], s.dd["dbg_" + n])
        if "p2" in s.debug:
            for n, t_ in (("ya_b", s.ya_b), ("yb_b", s.yb_b), ("cosT", s.cosT), ("qr", s.qr), ("cvo", s.cvo)):
                dn = s.dout("dbg_" + n, list(t_.shape), t_.dtype)
                P.dma("sp", dn.ap(), t_.ap(), [s.dd[n]], [s.dd["dbg_" + n]], s.dd["dbg_" + n])
        if "p3" in s.debug:
            for n, t_ in (("yc_b", s.yc_b), ("yd_b", s.yd_b), ("hdir", s.hdir)):
                dn = s.dout("dbg_" + n, list(t_.shape), t_.dtype)
                P.dma("sp", dn.ap(), t_.ap(), [s.dd[n]], [s.dd["dbg_" + n]], s.dd["dbg_" + n])
        P.wait_all("sp", [s.dd[n] for n in s.outs])
        P.emit()
        return nc


_CACHE = {}


def kernel(**inputs):
    cfg = Cfg()
    if "b" not in _CACHE:
        b = Builder(cfg); b.build(); _CACHE["b"] = b
    b = _CACHE["b"]
    maps = make_in_maps(cfg, inputs, b)
    res = run_bass_kernel_spmd(b.nc, maps, core_ids=list(range(NCORES)))
    yT = np.concatenate([np.asarray(r["yT"]) for r in res.results], axis=0)
    return np.ascontiguousarray(yT.T)[None].astype(np.float32)


def make_in_maps(cfg, inp, b):
    c = cfg; D = c.D; T = c.SEQ; L = c.DEPTH; Dc = D // NCORES
    f32 = np.float32
    g = lambda k: np.asarray(inp[k])
    x = g("x")[0]; w_in = g("w_in")
    inv_freq = (10000.0 ** (-np.arange(0, 64, 2, dtype=np.float32) / 64)).astype(f32)
    invf = np.concatenate([inv_freq, inv_freq]).reshape(64, 1).astype(f32)
    rotm = np.zeros((64, 64), f32)
    for m in range(32):
        rotm[m + 32, m] = -1.0; rotm[m, m + 32] = 1.0
    sel12 = np.zeros((2 * NCORES, 2, 128), f32)
    sel12[0::2, 0, :] = 1.0; sel12[1::2, 1, :] = 1.0
    ii = np.arange(128)
    cmats = np.stack([np.eye(128, dtype=f32), (ii[:, None] <= ii[None, :]).astype(f32), (ii[:, None] >= ii[None, :]).astype(f32),
                      (ii[:, None] < ii[None, :]).astype(f32), (ii[:, None] > ii[None, :]).astype(f32)], axis=1)
    cmats = np.ascontiguousarray(cmats)
    pl = lambda v, n: np.ascontiguousarray(v.reshape(L, n, 128).transpose(2, 0, 1))
    maps = []
    for cid in range(NCORES):
        hc = cid % 4
        idx = []
        o = c.off
        nq, nk, nr = c.QR // NCORES, c.KVR // NCORES, c.ROPE // NCORES
        idx += list(range(o["cq"][0] + nq * cid, o["cq"][0] + nq * cid + nq))
        idx += list(range(o["ckv"][0] + nk * cid, o["ckv"][0] + nk * cid + nk))
        idx += list(range(o["kr"][0] + nr * cid, o["kr"][0] + nr * cid + nr))
        idx += list(range(o["agate"][0] + 256 * cid, o["agate"][0] + 256 * cid + 256))
        idx += list(range(o["bglu"][0] + 256 * cid, o["bglu"][0] + 256 * cid + 256))
        idx += list(range(o["bglu"][0] + c.BW + 256 * cid, o["bglu"][0] + c.BW + 256 * cid + 256))
        idx += list(range(o["bgate"][0] + 256 * cid, o["bgate"][0] + 256 * cid + 256))
        idx += list(range(o["mq"][0] + 256 * hc, o["mq"][0] + 256 * hc + 256))
        idx += list(range(o["mk"][0] + 256 * hc, o["mk"][0] + 256 * hc + 256))
        idx += list(range(o["mv"][0] + 512 * hc, o["mv"][0] + 512 * hc + 512))
        idx += list(range(o["mo"][0] + 512 * hc, o["mo"][0] + 512 * hc + 512))
        idx += list(range(o["mgate"][0] + 512 * hc, o["mgate"][0] + 512 * hc + 512))
        idx += [o["mif"][0] + hc, o["mif"][0] + 4 + hc, o["mif"][0] + 8 + hc, o["mif"][0] + 12 + hc]
        idx += list(range(o["dq"][0] + 128 * hc, o["dq"][0] + 128 * hc + 128))
        for br in range(4):
            idx += list(range(o["merge"][0] + br * D + Dc * cid, o["merge"][0] + br * D + Dc * cid + Dc))
        idx = np.asarray(idx)
        assert len(idx) == b.NC1
        h0, h1 = 2 * cid, 2 * cid + 1
        wuq = g("w_uq").reshape(L, c.QR, c.AH, 192)
        wukv = g("w_ukv").reshape(L, c.KVR, c.AH, 256)
        cw = g("conv_w")[:, :, 256 * cid:256 * cid + 256]
        m = {
            "xT": np.ascontiguousarray(x[:, Dc * cid:Dc * cid + Dc].T),
            "normg": pl(g("norm_g")[:, Dc * cid:Dc * cid + Dc], Dc // 128),
            "w1": np.ascontiguousarray(w_in[:, :, idx]),
            "pos": np.ascontiguousarray(g("positions").astype(np.int32).reshape(1, T)),
            "cqg": pl(g("mla_cq_norm_g"), c.QR // 128), "ckvg": pl(g("mla_ckv_norm_g"), c.KVR // 128),
            "wuq": np.ascontiguousarray(np.concatenate([wuq[:, :, h0, :], wuq[:, :, h1, :]], axis=-1)),
            "wukv": np.ascontiguousarray(np.concatenate([wukv[:, :, h0, :], wukv[:, :, h1, :]], axis=-1)),
            "invf": invf, "rotm": rotm, "sel12": sel12,
            "convw": np.ascontiguousarray(cw.reshape(L, c.CONVW, 2, 128).transpose(3, 0, 2, 1)),
            "convb": pl(g("conv_b")[:, 256 * cid:256 * cid + 256], 2),
            "lng": pl(g("conv_ln_g")[:, 256 * cid:256 * cid + 256], 2),
            "lnb": pl(g("conv_ln_b")[:, 256 * cid:256 * cid + 256], 2),
            "memT": np.ascontiguousarray(g("mem")[0].T), "memg": pl(g("mem_norm_g"), D // 128),
            "wmkv": np.ascontiguousarray(np.concatenate([g("w_mem_kv")[:, :, 128 * hc:128 * hc + 128],
                                                          g("w_mem_kv")[:, :, c.DW + 128 * hc:c.DW + 128 * hc + 128]], axis=-1)),
            "wpa": np.ascontiguousarray(g("w_proj_a")[:, :, Dc * cid:Dc * cid + Dc]),
            "wpb": np.ascontiguousarray(g("w_proj_b")[:, :, Dc * cid:Dc * cid + Dc]),
            "wpc": np.ascontiguousarray(g("w_proj_c")[:, :, Dc * cid:Dc * cid + Dc]),
            "wpd": np.ascontiguousarray(g("w_proj_d")[:, :, Dc * cid:Dc * cid + Dc]),
            "wout": np.ascontiguousarray(g("w_out")[:, :, Dc * cid:Dc * cid + Dc]),
            "fng": np.ascontiguousarray(g("final_norm_g")[Dc * cid:Dc * cid + Dc].reshape(Dc // 128, 128).T),
            "gb": np.ascontiguousarray(np.broadcast_to(g("mlstm_gate_b")[:, :, :, hc].reshape(L, 4)[None], (128, L, 4))),
            "mng": np.ascontiguousarray(np.broadcast_to(g("mlstm_norm_g")[:, 512 * hc:512 * hc + 512][None], (128, L, 512))),
            "cmats": cmats,
        }
        maps.append(m)
    return maps
```
